# Optimizing a Trainium2 kernel written in Bass

```python
import math
import jax, jax.numpy as jnp
from jax import lax
import numpy as np

D_MODEL = 4096
BATCH = 1
SEQ = 16384
DEPTH = 1

N_META = 16
GRID_W = 64
Q_BLOCK = 128
HEAD_DIM = 128
N_Q_HEADS = 16
N_KV_HEADS = 4
GQA_GROUP = N_Q_HEADS // N_KV_HEADS
ATTN_WIDTH = N_Q_HEADS * HEAD_DIM
KV_WIDTH = N_KV_HEADS * HEAD_DIM
ROPE_THETA = 10000.0
ROPE_AXIS_DIM = HEAD_DIM // 2
ROPE_HALF = ROPE_AXIS_DIM // 2
HYENA_WIDTH = D_MODEL // 2
HYENA_ORDER = 2
HYENA_SHORT_CONV = 3
FILTER_EMB_DIM = 33
FILTER_BANDS = (FILTER_EMB_DIM - 1) // 2
FILTER_HIDDEN = 64
FILTER_OUT_SCALE = 0.005
DECAY_TARGET = 1e-2
FAST_DECAY_PCT = 0.3
SLOW_DECAY_PCT = 1.5
MIN_DECAY = math.log(DECAY_TARGET) / SLOW_DECAY_PCT
MAX_DECAY = math.log(DECAY_TARGET) / FAST_DECAY_PCT
D_FF = 11008
FFN_CONV = 3
NORM_EPS = 1e-6
IN_WIDTH = ATTN_WIDTH + 2 * KV_WIDTH + 3 * HYENA_WIDTH + 2 * D_MODEL
SPLIT_POINTS = (ATTN_WIDTH,
                ATTN_WIDTH + KV_WIDTH,
                ATTN_WIDTH + 2 * KV_WIDTH,
                ATTN_WIDTH + 2 * KV_WIDTH + 3 * HYENA_WIDTH,
                ATTN_WIDTH + 2 * KV_WIDTH + 3 * HYENA_WIDTH + D_MODEL)

kernel_name = "hybrid_gqa_hyena_gated_encoder_block"


def rmsnorm(x, g):
    xf = x.astype(jnp.float32)
    y = xf * lax.rsqrt(jnp.mean(xf * xf, axis=-1, keepdims=True) + NORM_EPS)
    return (y * g.astype(jnp.float32)).astype(x.dtype)


def depthwise_conv(x, w, b):
    width, ch = w.shape
    pad = (width - 1) // 2
    y = lax.conv_general_dilated(
        x, w[:, None, :].astype(x.dtype), window_strides=(1,), padding=[(pad, pad)],
        dimension_numbers=("NWC", "WIO", "NWC"), feature_group_count=ch)
    return y + b.astype(x.dtype)


def axial_rope_tables(rows):
    row = jnp.repeat(jnp.arange(rows, dtype=jnp.float32), GRID_W)
    col = jnp.tile(jnp.arange(GRID_W, dtype=jnp.float32), rows)
    meta = jnp.zeros((N_META,), jnp.float32)
    row = jnp.concatenate([meta, row])
    col = jnp.concatenate([meta, col])
    inv_freq = ROPE_THETA ** (-jnp.arange(ROPE_HALF, dtype=jnp.float32) * 2.0 / ROPE_AXIS_DIM)
    ang = jnp.stack([row[:, None] * inv_freq, col[:, None] * inv_freq], axis=1)
    return jnp.cos(ang), jnp.sin(ang)


def apply_rope(x, cos, sin):
    xf = x.astype(jnp.float32).reshape(*x.shape[:-1], 2, 2, ROPE_HALF)
    x1, x2 = xf[..., 0, :], xf[..., 1, :]
    c, s = cos[None, :, None], sin[None, :, None]
    out = jnp.stack([x1 * c - x2 * s, x2 * c + x1 * s], axis=-2)
    return out.reshape(x.shape).astype(x.dtype)


def gqa_attention(q, k, v):
    b, seq_len = q.shape[0], q.shape[1]
    n_real = seq_len - N_META
    n_blk = n_real // Q_BLOCK
    scale = HEAD_DIM ** -0.5
    q = q.reshape(b, seq_len, N_KV_HEADS, GQA_GROUP, HEAD_DIM)

    def attend(qb):
        s = jnp.einsum("btkgd,bskd->bkgts", qb, k).astype(jnp.float32) * scale
        p = jax.nn.softmax(s, axis=-1).astype(v.dtype)
        return jnp.einsum("bkgts,bskd->btkgd", p, v)

    o_meta = attend(q[:, :N_META])
    q_blocks = jnp.moveaxis(
        q[:, N_META:].reshape(b, n_blk, Q_BLOCK, N_KV_HEADS, GQA_GROUP, HEAD_DIM), 1, 0)
    o_real = jnp.moveaxis(lax.map(attend, q_blocks), 0, 1).reshape(
        b, n_real, N_KV_HEADS, GQA_GROUP, HEAD_DIM)
    return jnp.concatenate([o_meta, o_real], axis=1).reshape(b, seq_len, ATTN_WIDTH)


def implicit_filters(seq_len, w1, b1, w2, b2, w3, b3, w4, freq):
    dt = w1.dtype
    t = jnp.linspace(0.0, 1.0, seq_len, dtype=jnp.float32)[:, None]
    w = 2.0 * math.pi * jnp.arange(seq_len, dtype=jnp.float32)[:, None] / seq_len
    f = jnp.linspace(1e-4, FILTER_BANDS - 1, FILTER_BANDS, dtype=jnp.float32)[None, :]
    emb = jnp.concatenate([t, jnp.cos(f * w), -jnp.sin(f * w)], axis=-1).astype(dt)
    hid = jnp.sin(freq * (emb @ w1 + b1))
    hid = jnp.sin(freq * (hid @ w2 + b2))
    hid = jnp.sin(freq * (hid @ w3 + b3))
    filt = (hid @ w4).astype(jnp.float32).reshape(seq_len, HYENA_ORDER, 2, HYENA_WIDTH)
    deltas = jnp.linspace(MIN_DECAY, MAX_DECAY, HYENA_WIDTH, dtype=jnp.float32)
    decay = jnp.exp(-t * jnp.abs(deltas))
    return filt * decay[:, None, None, :]


def bidir_long_conv(z, h_fwd, h_bwd, d_skip):
    seq_len, ch = h_fwd.shape
    n_fft = 2 * seq_len
    k = jnp.concatenate([h_fwd, jnp.zeros((1, ch), h_fwd.dtype), h_bwd[:0:-1]], axis=0)
    k_f = jnp.fft.rfft(k, n=n_fft, axis=0)
    zf32 = z.astype(jnp.float32)
    z_f = jnp.fft.rfft(zf32, n=n_fft, axis=1)
    y = jnp.fft.irfft(z_f * k_f[None], n=n_fft, axis=1)[:, :seq_len]
    return (y + d_skip.astype(jnp.float32) * zf32).astype(z.dtype)


def setup_inputs(seed: int = 0) -> dict:
    key = jax.random.key(seed)
    ks = iter(jax.random.split(key, 32))

    def nrm(shape, scale):
        return jax.random.normal(next(ks), shape, jnp.float32) * scale

    def gain(shape):
        return 1.0 + 0.05 * jax.random.normal(next(ks), shape, jnp.float32)

    return {
        "x": nrm((BATCH, SEQ, D_MODEL), 1.0),
        "meta_tokens": nrm((N_META, D_MODEL), 1.0),
        "norm_mix": gain((DEPTH, D_MODEL)),
        "w_in": nrm((DEPTH, D_MODEL, IN_WIDTH), D_MODEL ** -0.5),
        "q_norm": gain((DEPTH, HEAD_DIM)),
        "k_norm": gain((DEPTH, HEAD_DIM)),
        "hyena_conv_w": nrm((DEPTH, HYENA_SHORT_CONV, 3 * HYENA_WIDTH), HYENA_SHORT_CONV ** -0.5),
        "hyena_conv_b": nrm((DEPTH, 3 * HYENA_WIDTH), 0.02),
        "filt_w1": nrm((DEPTH, FILTER_EMB_DIM, FILTER_HIDDEN), FILTER_EMB_DIM ** -0.5),
        "filt_b1": nrm((DEPTH, FILTER_HIDDEN), 0.1),
        "filt_w2": nrm((DEPTH, FILTER_HIDDEN, FILTER_HIDDEN), FILTER_HIDDEN ** -0.5),
        "filt_b2": nrm((DEPTH, FILTER_HIDDEN), 0.1),
        "filt_w3": nrm((DEPTH, FILTER_HIDDEN, FILTER_HIDDEN), FILTER_HIDDEN ** -0.5),
        "filt_b3": nrm((DEPTH, FILTER_HIDDEN), 0.1),
        "filt_w4": nrm((DEPTH, FILTER_HIDDEN, HYENA_ORDER * 2 * HYENA_WIDTH), FILTER_OUT_SCALE),
        "filt_freq": gain((DEPTH, FILTER_HIDDEN)),
        "hyena_skip": nrm((DEPTH, HYENA_ORDER, HYENA_WIDTH), 0.5),
        "w_attn_branch": nrm((DEPTH, ATTN_WIDTH, D_MODEL), ATTN_WIDTH ** -0.5),
        "w_hyena_branch": nrm((DEPTH, HYENA_WIDTH, D_MODEL), HYENA_WIDTH ** -0.5),
        "w_out": nrm((DEPTH, D_MODEL, D_MODEL), D_MODEL ** -0.5),
        "norm_ffn": gain((DEPTH, D_MODEL)),
        "w_ffn_gate": nrm((DEPTH, D_MODEL, D_FF), D_MODEL ** -0.5),
        "w_ffn_up": nrm((DEPTH, D_MODEL, D_FF), D_MODEL ** -0.5),
        "ffn_conv_w": nrm((DEPTH, FFN_CONV, D_FF), FFN_CONV ** -0.5),
        "ffn_conv_b": nrm((DEPTH, D_FF), 0.02),
        "w_ffn_down": nrm((DEPTH, D_FF, D_MODEL), D_FF ** -0.5),
        "norm_final": gain((D_MODEL,)),
    }


def reference(x, meta_tokens, norm_mix, w_in, q_norm, k_norm, hyena_conv_w, hyena_conv_b,
              filt_w1, filt_b1, filt_w2, filt_b2, filt_w3, filt_b3, filt_w4, filt_freq,
              hyena_skip, w_attn_branch, w_hyena_branch, w_out, norm_ffn, w_ffn_gate,
              w_ffn_up, ffn_conv_w, ffn_conv_b, w_ffn_down, norm_final):
    batch, n_tok, _ = x.shape
    rows = n_tok // GRID_W
    seq_len = n_tok + N_META
    meta = jnp.broadcast_to(meta_tokens[None].astype(x.dtype), (batch, N_META, D_MODEL))
    h = jnp.concatenate([meta, x], axis=1)
    cos, sin = axial_rope_tables(rows)

    for i in range(DEPTH):
        u = rmsnorm(h, norm_mix[i])
        proj = u @ w_in[i]
        q, k, v, hy, g_attn, g_hy = jnp.split(proj, SPLIT_POINTS, axis=-1)

        q = q.reshape(batch, seq_len, N_Q_HEADS, HEAD_DIM)
        k = k.reshape(batch, seq_len, N_KV_HEADS, HEAD_DIM)
        v = v.reshape(batch, seq_len, N_KV_HEADS, HEAD_DIM)
        q = apply_rope(rmsnorm(q, q_norm[i]), cos, sin)
        k = apply_rope(rmsnorm(k, k_norm[i]), cos, sin)
        y_attn = gqa_attention(q, k, v)

        hy = depthwise_conv(hy, hyena_conv_w[i], hyena_conv_b[i])
        z, x1, x2 = jnp.split(hy, 3, axis=-1)
        filters = implicit_filters(seq_len, filt_w1[i], filt_b1[i], filt_w2[i], filt_b2[i],
                                   filt_w3[i], filt_b3[i], filt_w4[i], filt_freq[i])
        for o, gate in enumerate((x1, x2)):
            z = gate * bidir_long_conv(z, filters[:, o, 0], filters[:, o, 1], hyena_skip[i, o])

        mixed = (jax.nn.sigmoid(g_attn) * (y_attn @ w_attn_branch[i])
                 + jax.nn.sigmoid(g_hy) * (z @ w_hyena_branch[i]))
        h = h + mixed @ w_out[i]

        u2 = rmsnorm(h, norm_ffn[i])
        gate = depthwise_conv(u2 @ w_ffn_gate[i], ffn_conv_w[i], ffn_conv_b[i])
        h = h + (jax.nn.silu(gate) * (u2 @ w_ffn_up[i])) @ w_ffn_down[i]

    return rmsnorm(h, norm_final)[:, N_META:]
```

```python
import numpy as np, math
from contextlib import ExitStack
import concourse.bass as bass
import concourse.mybir as mybir
from concourse.bass_utils import run_bass_kernel_spmd
import ml_dtypes

F32 = mybir.dt.float32
BF16 = mybir.dt.bfloat16
AF = mybir.ActivationFunctionType
ALU = mybir.AluOpType
AX = mybir.AxisListType
NPBF = ml_dtypes.bfloat16


class KB:
    ENG = ['pe', 'act', 'dve', 'pool', 'sp']
    LIM = 30000
    RING = 6

    def __init__(self, nc, stack):
        self.nc = nc
        self.stack = stack
        self.ops = {e: [] for e in self.ENG}
        self.cnt = {e: 0 for e in self.ENG}
        self.cursem = {e: None for e in self.ENG}
        self.res = {}
        self.seen = {e: {} for e in self.ENG}
        self.ring = {q: [] for q in ('sp', 'act', 'pool')}
        self.ridx = {q: 0 for q in ('sp', 'act', 'pool')}
        self.nsem = 0
        self.nbuf = 0
        self.out_tokens = []
        self.marks = []
        self.npe = 0

    def newsem(self):
        self.nsem += 1
        return self.stack.enter_context(self.nc.semaphore(f"s{self.nsem}"))

    def sb(self, shape, dt, name=None):
        self.nbuf += 1
        return self.stack.enter_context(self.nc.sbuf_tensor(name or f"b{self.nbuf}", list(shape), dt))

    def ps(self, shape, dt=F32, name=None):
        self.nbuf += 1
        return self.stack.enter_context(self.nc.psum_tensor(name or f"p{self.nbuf}", list(shape), dt))

    def _deps(self, eng, reads, writes):
        need = {}

        def add(tok, kind):
            sem, val, peng = tok
            if peng == eng:
                if eng == 'pe' or kind != 'raw':
                    return
            k = id(sem)
            if self.seen[eng].get(k, 0) >= val:
                return
            if k not in need or need[k][1] < val:
                need[k] = (sem, val)

        for r in reads:
            st = self.res.get(r)
            if st:
                for t in st[0].values():
                    add(t, 'raw')
        for w in writes:
            st = self.res.get(w)
            if st:
                for t in st[0].values():
                    add(t, 'waw')
                for t in st[1].values():
                    add(t, 'war')
        for k, (sem, val) in need.items():
            self.seen[eng][k] = val
        return list(need.values())

    def _commit(self, tok, reads, writes):
        k = id(tok[0])
        for r in reads:
            st = self.res.setdefault(r, [{}, {}])
            if k not in st[1] or st[1][k][1] < tok[1]:
                st[1][k] = tok
        for w in writes:
            st = self.res.setdefault(w, [{}, {}])
            if k not in st[0] or st[0][k][1] < tok[1]:
                st[0][k] = tok

    def mark(self, label):
        self.marks.append((label, self.npe))

    def op(self, eng, fn, reads=(), writes=()):
        if eng == 'pe':
            self.npe += 1
        if self.cursem[eng] is None or self.cnt[eng] >= self.LIM:
            self.cursem[eng] = self.newsem()
            self.cnt[eng] = 0
        sem = self.cursem[eng]
        self.cnt[eng] += 1
        tok = (sem, self.cnt[eng], eng)
        waits = self._deps(eng, reads, writes)
        self.ops[eng].append(('op', waits, fn, sem))
        self._commit(tok, reads, writes)
        return tok

    def dma(self, q, out, in_, reads=(), writes=(), is_output=False, **kw):
        ring = self.ring[q]
        if len(ring) < self.RING:
            ring.append([self.newsem(), 0])
            slot = ring[-1]
        else:
            i = self.ridx[q] % self.RING
            self.ridx[q] += 1
            slot = ring[i]
            if slot[1] >= 1800:
                ring[i] = slot = [self.newsem(), 0]
        waits = self._deps(q, reads, writes)
        if slot[1] > 0:
            k = id(slot[0])
            v = 16 * slot[1]
            if self.seen[q].get(k, 0) < v:
                self.seen[q][k] = v
                waits.append((slot[0], v))
        slot[1] += 1
        tok = (slot[0], 16 * slot[1], 'dma')
        self.ops[q].append(('dma', waits, out, in_, slot[0], kw))
        self._commit(tok, reads, writes)
        if is_output:
            self.out_tokens.append(tok)
        return tok

    def emit(self):
        fin = {}
        for sem, val, _ in self.out_tokens:
            k = id(sem)
            if k not in fin or fin[k][1] < val:
                fin[k] = (sem, val)
        self.ops['sp'].append(('wait', list(fin.values())))
        block = self.stack.enter_context(self.nc.Block())
        for eng, deco in (('pe', block.tensor), ('act', block.scalar), ('dve', block.vector),
                          ('pool', block.gpsimd), ('sp', block.sync)):
            ops = self.ops[eng]

            def body(e, ops=ops):
                for item in ops:
                    for sem, val in item[1]:
                        e.wait_ge(sem, val)
                    if item[0] == 'op':
                        item[2](e).then_inc(item[3], 1)
                    elif item[0] == 'dma':
                        e.dma_start(out=item[2], in_=item[3], **item[5]).then_inc(item[4], 16)
            deco(body)


N1, N2 = 128, 384
NFFT = N1 * N2
MIN_DECAY = math.log(1e-2) / 1.5
MAX_DECAY = math.log(1e-2) / 0.3


def hy_consts(L):
    NR = (L + N2 - 1) // N2
    c = {}
    n1 = np.arange(NR)[:, None]; k1 = np.arange(N1)[None, :]
    a = 2 * np.pi * n1 * k1 / N1
    c["F1"] = np.concatenate([np.cos(a), -np.sin(a)], 1).astype(NPBF)
    n2 = (np.arange(3)[None, :, None] * 128 + np.arange(128)[:, None, None])
    a = 2 * np.pi * n2 * np.arange(N1)[None, None, :] / NFFT
    twre, twim = np.cos(a), -np.sin(a)
    c["TWA"] = np.tile(twre, (1, 1, 4)).astype(np.float32)
    c["TWB"] = np.tile(twim, (1, 1, 4)).astype(np.float32)
    a = 2 * np.pi * n2 * np.arange(N2)[None, None, :] / N2
    fre, fim = np.cos(a), -np.sin(a)
    c["F2"] = np.stack([fre, fim, -fim, -fre], 1).astype(NPBF)
    gre, gim = np.cos(a), np.sin(a)
    c["G2"] = np.stack([gre, gim, -gim], 1).astype(NPBF)
    a = 2 * np.pi * np.arange(N1)[:, None] * np.arange(N2)[None, :] / NFFT
    c["TWI"] = np.stack([np.cos(a), np.sin(a)], 1).astype(np.float32)
    a = 2 * np.pi * np.arange(N1)[:, None] * np.arange(NR)[None, :] / N1
    c["G1"] = np.stack([np.cos(a) / NFFT, -np.sin(a) / NFFT], 1).astype(NPBF)
    t = np.linspace(0.0, 1.0, L, dtype=np.float32)
    w = (2.0 * np.pi * np.arange(L, dtype=np.float32) / L).astype(np.float32)
    f = np.linspace(1e-4, 15.0, 16, dtype=np.float32)
    emb = np.concatenate([t[:, None], np.cos(f[None, :] * w[:, None]), -np.sin(f[None, :] * w[:, None])], -1).astype(np.float32)
    c["embT"] = np.ascontiguousarray(emb.T)
    c["tv"] = np.ascontiguousarray(np.broadcast_to(t[None, :], (128, L))).astype(np.float32)
    return c


def hy_delta(core, width=2048):
    d = np.abs(np.linspace(MIN_DECAY, MAX_DECAY, width, dtype=np.float32))[core * 256:(core + 1) * 256]
    return np.ascontiguousarray((-d).reshape(2, 128).T).astype(np.float32)


def hyena_phase(nc, kb, psb, hy_s, dr, L, debug=False):
    NR = (L + N2 - 1) // N2
    LP = NR * N2
    NG4 = 32
    F1_d = dr("F1", [NR, 256], BF16); TWA_d = dr("TWA", [128, 3, 512]); TWB_d = dr("TWB", [128, 3, 512])
    F2_d = dr("F2", [128, 4, 3, 384], BF16); G2_d = dr("G2", [128, 3, 3, 384], BF16)
    TWI_d = dr("TWI", [128, 2, 384]); G1_d = dr("G1", [128, 2, NR], BF16)
    emb_d = dr("embT", [33, L]); tv_d = dr("tv", [128, L]); nd_d = dr("nabsd", [128, 2])
    fw1_d = dr("fw1", [33, 64]); fw2_d = dr("fw2", [64, 64]); fw3_d = dr("fw3", [64, 64]); fw4_d = dr("fw4", [64, 1024])
    fb_d = dr("fbf", [64, 4])
    sk_d = dr("skip", [128, 4])
    hcw_d = dr("hcw", [128, 18]); hcb_d = dr("hcb", [128, 6])
    z2_d = dr("z2T", [256, L], BF16, "ExternalOutput")
    zc_s = [dr(f"zc{o}_s", [256, LP], BF16, "Internal") for o in range(2)]
    xc_s = [dr(f"xc{o}_s", [256, LP], F32, "Internal") for o in range(2)]
    hf_s = dr("hf_s", [8, 128, LP], BF16, "ExternalOutput" if debug else "Internal")
    K_s = dr("K_s", [2, 2, NG4, 128, 3072], F32, "Internal")
    if debug:
        zdbg = dr("zdbg", [256, LP], BF16, "ExternalOutput")

    rr = [0]
    held = set()

    def bank(hold=False):
        for _ in range(8):
            b = rr[0] % 8
            rr[0] += 1
            if b not in held:
                if hold:
                    held.add(b)
                return b
        raise AssertionError("all PSUM banks held")

    def release(*bs):
        for b in bs:
            held.discard(b)

    def run_interleaved(gens):
        active = list(gens)
        while active:
            for g in list(active):
                try:
                    next(g)
                except StopIteration:
                    active.remove(g)

    with ExitStack() as st:
        cnt = [0]

        def sb(shape, dt):
            cnt[0] += 1
            return st.enter_context(nc.sbuf_tensor(f"hy{cnt[0]}", list(shape), dt))

        allres = []

        LIVE = ('ones', 'onesb', 'eps', 'zer', 'F1', 'TWA', 'TWB', 'F2', 'G2', 'TWI', 'G1', 'nd', 'fbf', 'sk', 'hcw', 'hcb',
                'negpi', 'zpad', 'zpadb')
        wr_seen = set()

        def refresh():
            allres[:] = [k for k in kb.res.keys() if not k.startswith('ps') and not k.endswith('_s') and k not in LIVE]
            wr_seen.clear()
        refresh()
        st0 = st.enter_context(ExitStack())

        def sb0(shape, dt):
            cnt[0] += 1
            return st0.enter_context(nc.sbuf_tensor(f"hy{cnt[0]}", list(shape), dt))

        def wr(names):
            names = list(names)
            if names[0] in wr_seen:
                return names
            wr_seen.add(names[0])
            return names + allres

        F1 = sb([NR, 256], BF16); TWA = sb([128, 3, 512], F32); TWB = sb([128, 3, 512], F32)
        F2 = sb([128, 4, 3, 384], BF16); G2 = sb([128, 3, 3, 384], BF16); TWI = sb([128, 2, 384], F32); G1 = sb([128, 2, NR], BF16)
        for t_, d_, n_ in ((F1, F1_d, 'F1'), (TWA, TWA_d, 'TWA'), (TWB, TWB_d, 'TWB'), (F2, F2_d, 'F2'), (G2, G2_d, 'G2'), (TWI, TWI_d, 'TWI'), (G1, G1_d, 'G1')):
            kb.dma('sp', t_[:], d_, writes=wr([n_]))
        negpi = sb([128, 1], F32)
        kb.op('pool', lambda e: e.memset(negpi[:], -math.pi), writes=wr(['negpi']))
        zpad = sb([128, N2], F32)
        kb.op('pool', lambda e: e.memset(zpad[:], 0.0), writes=wr(['zpad']))
        zpadb = sb([128, N2], BF16)
        kb.op('pool', lambda e: e.memset(zpadb[:], 0.0), writes=wr(['zpadb']))
        nd = sb([128, 2], F32); fbf = sb([64, 4], F32); sk = sb([128, 4], F32); hcw = sb([128, 18], F32); hcb = sb([128, 6], F32)
        fw1 = sb0([33, 64], F32); fw2 = sb0([64, 64], F32); fw3 = sb0([64, 64], F32); fw4 = sb0([64, 1024], F32)
        for t_, d_, n_ in ((nd, nd_d, 'nd'), (fbf, fb_d, 'fbf'), (sk, sk_d, 'sk'), (hcw, hcw_d, 'hcw'), (hcb, hcb_d, 'hcb'),
                           (fw1, fw1_d, 'fw1'), (fw2, fw2_d, 'fw2'), (fw3, fw3_d, 'fw3'), (fw4, fw4_d, 'fw4')):
            kb.dma('sp', t_[:], d_, writes=wr([n_]))

        kb.mark('H0')
        CH = 2048
        hin = [sb0([128, CH + 2], F32) for _ in range(2)]
        hout = [sb0([128, CH], F32) for _ in range(2)]
        houtb = [sb0([128, CH], BF16) for _ in range(2)]
        it = 0
        for j in range(6):
            for c0 in range(0, L, CH):
                n = min(CH, L - c0)
                i = it % 2
                it += 1
                kb.dma('sp', hin[i][:, 0:n + 2], hy_s[j * 128:(j + 1) * 128, c0:c0 + n + 2], reads=['hy_s'], writes=wr([f'hin{i}']))
                kb.op('dve', lambda e, i=i, n=n, j=j: e.tensor_scalar(out=hout[i][:, 0:n], in0=hin[i][:, 1:n + 1], scalar1=hcw[:, 3 * j + 1:3 * j + 2],
                                                                      scalar2=hcb[:, j:j + 1], op0=ALU.mult, op1=ALU.add),
                      reads=[f'hin{i}', 'hcw', 'hcb'], writes=wr([f'hout{i}']))
                kb.op('dve', lambda e, i=i, n=n, j=j: e.scalar_tensor_tensor(out=hout[i][:, 0:n], in0=hin[i][:, 0:n], scalar=hcw[:, 3 * j:3 * j + 1],
                                                                             in1=hout[i][:, 0:n], op0=ALU.mult, op1=ALU.add),
                      reads=[f'hin{i}', f'hout{i}'], writes=[f'hout{i}'])
                if j < 2:
                    kb.op('dve', lambda e, i=i, n=n, j=j: e.scalar_tensor_tensor(out=houtb[i][:, 0:n], in0=hin[i][:, 2:n + 2], scalar=hcw[:, 3 * j + 2:3 * j + 3],
                                                                                 in1=hout[i][:, 0:n], op0=ALU.mult, op1=ALU.add),
                          reads=[f'hin{i}', f'hout{i}'], writes=wr([f'houtb{i}']))
                    kb.dma('sp', zc_s[0][j * 128:(j + 1) * 128, c0:c0 + n], houtb[i][:, 0:n], reads=[f'houtb{i}'], writes=['zc0_s'])
                else:
                    kb.op('dve', lambda e, i=i, n=n, j=j: e.scalar_tensor_tensor(out=hout[i][:, 0:n], in0=hin[i][:, 2:n + 2], scalar=hcw[:, 3 * j + 2:3 * j + 3],
                                                                                 in1=hout[i][:, 0:n], op0=ALU.mult, op1=ALU.add),
                          reads=[f'hin{i}', f'hout{i}'], writes=[f'hout{i}'])
                    o = (j - 2) // 2
                    ctt = (j - 2) % 2
                    kb.dma('sp', xc_s[o][ctt * 128:(ctt + 1) * 128, c0:c0 + n], hout[i][:, 0:n], reads=[f'hout{i}'], writes=[f'xc{o}_s'])
        if LP > L:
            for j in range(2):
                kb.dma('sp', zc_s[0][j * 128:(j + 1) * 128, L:LP], zpadb[:, 0:LP - L], reads=['zpadb'], writes=['zc0_s'])
                for o in range(2):
                    kb.dma('sp', xc_s[o][j * 128:(j + 1) * 128, L:LP], zpad[:, 0:LP - L], reads=['zpad'], writes=[f'xc{o}_s'])
            for r in range(8):
                kb.dma('sp', hf_s[r, :, L:LP], zpadb[:, 0:LP - L], reads=['zpadb'], writes=['hf_s'])

        kb.mark('H1')
        TC = 512
        embt = [sb0([33, TC], F32) for _ in range(2)]
        tvt = [sb0([128, TC], F32) for _ in range(2)]
        hid = [sb0([64, TC], F32) for _ in range(3)]
        dec = [sb0([128, TC], F32) for _ in range(2)]
        fo = [sb0([128, TC], BF16) for _ in range(3)]
        C1 = math.pi + 16 * math.pi
        fi = 0
        for ci, c0 in enumerate(range(0, L, TC)):
            n = min(TC, L - c0)
            i = ci % 2
            kb.dma('sp', embt[i][:, 0:n], emb_d[:, c0:c0 + n], writes=wr([f'embt{i}']))
            kb.dma('sp', tvt[i][:, 0:n], tv_d[:, c0:c0 + n], writes=wr([f'tvt{i}']))
            src, sres, K = embt[i], f'embt{i}', 33
            for li, wl in enumerate((fw1, fw2, fw3)):
                b = bank()
                kb.op('pe', lambda e, b=b, wl=wl, src=src, K=K, n=n: e.matmul(psb[b][0:64, 0:n], lhsT=wl[0:K, :], rhs=src[0:K, 0:n], start=True, stop=True),
                      reads=[f'fw{li + 1}', sres], writes=[f'ps{b}'])
                h = hid[li % 2]
                hres = f'hid{li % 2}'
                kb.op('dve', lambda e, b=b, h=h, li=li, n=n: e.tensor_scalar(out=h[:, 0:n], in0=psb[b][0:64, 0:n], scalar1=fbf[:, li:li + 1], scalar2=fbf[:, 3:4],
                                                                           op0=ALU.add, op1=ALU.mult),
                      reads=[f'ps{b}', 'fbf'], writes=wr([hres]))
                h2 = hid[2]
                MAGIC = 12582912.0
                kb.op('dve', lambda e, h=h, h2=h2, n=n: e.tensor_scalar(out=h2[:, 0:n], in0=h[:, 0:n], scalar1=1.0 / (2 * math.pi), scalar2=MAGIC, op0=ALU.mult, op1=ALU.add),
                      reads=[hres], writes=wr(['hid2']))
                kb.op('dve', lambda e, h2=h2, n=n: e.tensor_scalar(out=h2[:, 0:n], in0=h2[:, 0:n], scalar1=-MAGIC, scalar2=None, op0=ALU.add),
                      reads=['hid2'], writes=['hid2'])
                kb.op('dve', lambda e, h=h, h2=h2, n=n: e.scalar_tensor_tensor(out=h[:, 0:n], in0=h2[:, 0:n], scalar=-2 * math.pi, in1=h[:, 0:n], op0=ALU.mult, op1=ALU.add),
                      reads=['hid2', hres], writes=[hres])
                kb.op('dve', lambda e, h=h, n=n: e.tensor_scalar(out=h[:, 0:n], in0=h[:, 0:n], scalar1=-3.1415925, scalar2=3.1415925, op0=ALU.max, op1=ALU.min),
                      reads=[hres], writes=[hres])
                kb.op('act', lambda e, h=h, n=n: e.activation(out=h[:, 0:n], in_=h[:, 0:n], func=AF.Sin),
                      reads=[hres], writes=[hres])
                src, sres, K = h, hres, 64
            for ct in range(2):
                kb.op('act', lambda e, ct=ct, i=i, n=n: e.activation(out=dec[ct][:, 0:n], in_=tvt[i][:, 0:n], func=AF.Exp, scale=nd[:, ct:ct + 1]),
                      reads=[f'tvt{i}', 'nd'], writes=wr([f'dec{ct}']))
            for o in range(2):
                for d in range(2):
                    for ct in range(2):
                        r = (o * 2 + d) * 2 + ct
                        b = bank()
                        kb.op('pe', lambda e, b=b, r=r, src=src, n=n: e.matmul(psb[b][:, 0:n], lhsT=fw4[:, r * 128:(r + 1) * 128], rhs=src[:, 0:n], start=True, stop=True),
                              reads=['fw4', sres], writes=[f'ps{b}'])
                        f = fo[fi % 3]; fres = f'fo{fi % 3}'
                        fi += 1
                        if ci == 0:
                            kb.op('dve', lambda e, b=b, ct=ct, n=n: e.tensor_tensor(out=dec[ct][:, 0:n], in0=psb[b][:, 0:n], in1=dec[ct][:, 0:n], op=ALU.mult),
                                  reads=[f'ps{b}', f'dec{ct}'], writes=[f'dec{ct}'])
                            if d == 0:
                                kb.op('dve', lambda e, ct=ct, o=o: e.tensor_tensor(out=dec[ct][:, 0:1], in0=dec[ct][:, 0:1], in1=sk[:, o * 2 + ct:o * 2 + ct + 1], op=ALU.add),
                                      reads=[f'dec{ct}', 'sk'], writes=[f'dec{ct}'])
                            else:
                                kb.op('dve', lambda e, ct=ct: e.memset(dec[ct][:, 0:1], 0.0), reads=[f'dec{ct}'], writes=[f'dec{ct}'])
                            kb.op('dve', lambda e, f=f, ct=ct, n=n: e.tensor_copy(out=f[:, 0:n], in_=dec[ct][:, 0:n]), reads=[f'dec{ct}'], writes=wr([fres]))
                            kb.op('act', lambda e, ct=ct, i=i, n=n: e.activation(out=dec[ct][:, 0:n], in_=tvt[i][:, 0:n], func=AF.Exp, scale=nd[:, ct:ct + 1]),
                                  reads=[f'tvt{i}', 'nd', fres], writes=[f'dec{ct}'])
                        else:
                            kb.op('dve', lambda e, b=b, f=f, ct=ct, n=n: e.tensor_tensor(out=f[:, 0:n], in0=psb[b][:, 0:n], in1=dec[ct][:, 0:n], op=ALU.mult),
                                  reads=[f'ps{b}', f'dec{ct}'], writes=wr([fres]))
                        kb.dma('sp', hf_s[r, :, c0:c0 + n], f[:, 0:n], reads=[fres], writes=['hf_s'], is_output=debug)

        st0.close()
        refresh()
        zt = [sb([NR, 4, N2], BF16) for _ in range(4)]
        m = [sb([128, 512], F32) for _ in range(8)]
        Bb = [sb([128, 3, 2, 4, 128], BF16) for _ in range(4)]
        NKT_ = 3
        Kt = [sb([128, 3072], F32) for _ in range(NKT_)]
        Pb = [sb([128, 3, 2, 4, 128], BF16) for _ in range(2)]
        Eb = [sb([128, 2, 4, N2], BF16) for _ in range(2)]
        xt = [sb([NR, 4, N2], F32) for _ in range(3)]
        zo = [sb([NR, 4, N2], BF16) for _ in range(2)]
        ctr = dict(zt=0, m=0, B=0, K=0, P=0, E=0, x=0, zo=0)

        def nxt(name, n):
            v = ctr[name] % n
            ctr[name] += 1
            return v

        def mtile():
            i = nxt('m', 8)
            return m[i], f'm{i}'

        def load_zt(src_rows, sres):
            i = nxt('zt', 4)
            kb.dma('sp', zt[i][:], src_rows.rearrange("c (a b) -> a c b", b=N2), reads=[sres], writes=wr([f'zt{i}']))
            return zt[i], f'zt{i}'

        def fwd_S1_TW(z, zres):
            bi = nxt('B', 4)
            B = Bb[bi]; bres = f'B{bi}'
            for j in range(3):
                for cp in range(2):
                    b = bank()
                    for cc in range(2):
                        c = cp * 2 + cc
                        kb.op('pe', lambda e, b=b, cc=cc, c=c, j=j: e.matmul(psb[b][:, cc * 256:(cc + 1) * 256], lhsT=z[:, c, j * 128:(j + 1) * 128], rhs=F1[:, :],
                                                                            start=True, stop=True),
                              reads=[zres, 'F1'], writes=[f'ps{b}'])
                    m1, r1 = mtile(); m2, r2 = mtile()
                    kb.op('dve', lambda e, b=b, m1=m1, j=j: e.tensor_tensor(out=m1[:], in0=psb[b][:, :], in1=TWA[:, j, :], op=ALU.mult),
                          reads=[f'ps{b}', 'TWA'], writes=wr([r1]))
                    kb.op('dve', lambda e, b=b, m2=m2, j=j: e.tensor_tensor(out=m2[:], in0=psb[b][:, :], in1=TWB[:, j, :], op=ALU.mult),
                          reads=[f'ps{b}', 'TWB'], writes=wr([r2]))
                    v1 = m1[:].rearrange("p (c r k) -> p c r k", c=2, r=2)
                    v2 = m2[:].rearrange("p (c r k) -> p c r k", c=2, r=2)
                    kb.op('pool', lambda e, B=B, j=j, cp=cp, v1=v1, v2=v2: e.tensor_tensor(out=B[:, j, 0, cp * 2:cp * 2 + 2, :], in0=v1[:, :, 0, :], in1=v2[:, :, 1, :], op=ALU.subtract),
                          reads=[r1, r2], writes=wr([bres]))
                    kb.op('pool', lambda e, B=B, j=j, cp=cp, v1=v1, v2=v2: e.tensor_tensor(out=B[:, j, 1, cp * 2:cp * 2 + 2, :], in0=v2[:, :, 0, :], in1=v1[:, :, 1, :], op=ALU.add),
                          reads=[r1, r2], writes=[bres])
                    yield
            return B, bres

        def S2(terms, jj):
            bre, bim = bank(True), bank(True)
            nt = len(terms) * 3 * 2
            i = 0
            for (B, bres, sgn) in terms:
                for j in range(3):
                    pr = ((0, 0), (2, 1))
                    pi2 = ((1, 0), (0, 1)) if sgn > 0 else ((2, 0), (3, 1))
                    for t in range(2):
                        for out_b, (fidx, ri) in ((bre, pr[t]), (bim, pi2[t])):
                            kb.op('pe', lambda e, out_b=out_b, fidx=fidx, ri=ri, j=j, B=B, i=i: e.matmul(
                                psb[out_b][:, :], lhsT=F2[:, fidx, j, jj * 128:(jj + 1) * 128], rhs=B[:, j, ri, :, :],
                                start=(i == 0), stop=(i == nt - 1)),
                                reads=['F2', bres], writes=[f'ps{out_b}'])
                        i += 1
                    yield
            return bre, bim

        kb.mark('H2')
        def h2_A(ctx):
            o, ct, g = ctx['key']
            zf, zfr = load_zt(hf_s[(o * 2 + 0) * 2 + ct, g * 4:(g + 1) * 4, :], 'hf_s')
            zb, zbr = load_zt(hf_s[(o * 2 + 1) * 2 + ct, g * 4:(g + 1) * 4, :], 'hf_s')
            ctx['Bf'] = yield from fwd_S1_TW(zf, zfr)
            ctx['Bk'] = yield from fwd_S1_TW(zb, zbr)

        def h2_B(ctx):
            o, ct, g = ctx['key']
            (Bf, bfr), (Bk, bkr) = ctx['Bf'], ctx['Bk']
            ki = nxt('K', NKT_)
            K = Kt[ki]; kres = f'K{ki}'
            for jj in range(3):
                bre, bim = yield from S2([(Bf, bfr, 1), (Bk, bkr, -1)], jj)
                kb.op('act', lambda e, K=K, jj=jj, bre=bre: e.copy(out=K[:, (jj * 2) * 512:(jj * 2 + 1) * 512], in_=psb[bre][:, :]),
                      reads=[f'ps{bre}'], writes=wr([kres]))
                kb.op('act', lambda e, K=K, jj=jj, bim=bim: e.copy(out=K[:, (jj * 2 + 1) * 512:(jj * 2 + 2) * 512], in_=psb[bim][:, :]),
                      reads=[f'ps{bim}'], writes=[kres])
                release(bre, bim)
                yield
            kb.dma('act', K_s[o, ct, g], K[:], reads=[kres], writes=['K_s'])

        keys = [(o, ct, g) for o in range(2) for ct in range(2) for g in range(NG4)]
        ctxs = [dict(key=k) for k in keys]
        for i in range(len(keys) + 1):
            gens = []
            if i < len(keys):
                gens.append(h2_A(ctxs[i]))
            if i - 1 >= 0:
                gens.append(h2_B(ctxs[i - 1]))
            run_interleaved(gens)

        kb.mark('H3')
        def h3_A(ctx):
            o, ct, g = ctx['key']
            rows = ctx['rows']
            z, zres = load_zt(zc_s[o][rows, :], f'zc{o}_s')
            ki = nxt('K', NKT_)
            K = Kt[ki]; kres = f'K{ki}'
            kb.dma('sp', K[:], K_s[o, ct, g], reads=['K_s'], writes=[kres])
            ctx['K'] = (K, kres)
            ctx['B'] = yield from fwd_S1_TW(z, zres)

        def h3_B(ctx):
            o, ct, g = ctx['key']
            rows = ctx['rows']
            K, kres = ctx['K']
            B, bres = ctx['B']
            xi = nxt('x', 3)
            X = xt[xi]; xres = f'x{xi}'
            kb.dma('sp', X[:], xc_s[o][rows, :].rearrange("c (a b) -> a c b", b=N2), reads=[f'xc{o}_s'], writes=wr([xres]))
            ctx['X'] = (X, xres)
            pi_ = nxt('P', 2)
            P = Pb[pi_]; pres = f'P{pi_}'
            ctx['P'] = (P, pres)
            for jj in range(3):
                bre, bim = yield from S2([(B, bres, 1)], jj)
                Kre = K[:, (jj * 2) * 512:(jj * 2 + 1) * 512]
                Kim = K[:, (jj * 2 + 1) * 512:(jj * 2 + 2) * 512]
                ms = [mtile() for _ in range(4)]
                for (mt, mr), (pb, kk) in zip(ms, ((bre, Kre), (bim, Kim), (bre, Kim), (bim, Kre))):
                    kb.op('dve', lambda e, mt=mt, pb=pb, kk=kk: e.tensor_tensor(out=mt[:], in0=psb[pb][:, :], in1=kk, op=ALU.mult),
                          reads=[f'ps{pb}', kres], writes=wr([mr]))
                kb.op('pool', lambda e, P=P, jj=jj, ms=ms: e.tensor_tensor(out=P[:, jj, 0, :, :], in0=ms[0][0][:].rearrange("p (c k) -> p c k", c=4),
                                                                         in1=ms[1][0][:].rearrange("p (c k) -> p c k", c=4), op=ALU.subtract),
                      reads=[ms[0][1], ms[1][1]], writes=wr([pres]))
                kb.op('pool', lambda e, P=P, jj=jj, ms=ms: e.tensor_tensor(out=P[:, jj, 1, :, :], in0=ms[2][0][:].rearrange("p (c k) -> p c k", c=4),
                                                                         in1=ms[3][0][:].rearrange("p (c k) -> p c k", c=4), op=ALU.add),
                      reads=[ms[2][1], ms[3][1]], writes=[pres])
                release(bre, bim)
                yield

        def h3_C(ctx):
            P, pres = ctx['P']
            ei = nxt('E', 2)
            E = Eb[ei]; eres = f'E{ei}'
            ctx['E'] = (E, eres)
            for c in range(4):
                bre, bim = bank(True), bank(True)
                i = 0
                for jj in range(3):
                    for t in range(2):
                        for out_b, (ri, gidx) in ((bre, ((0, 0), (1, 2))[t]), (bim, ((0, 1), (1, 0))[t])):
                            kb.op('pe', lambda e, out_b=out_b, jj=jj, ri=ri, gidx=gidx, c=c, i=i, P=P: e.matmul(
                                psb[out_b][:, 0:N2], lhsT=P[:, jj, ri, c, :], rhs=G2[:, gidx, jj, :], start=(i == 0), stop=(i == 5)),
                                reads=[pres, 'G2'], writes=[f'ps{out_b}'])
                        i += 1
                yield
                ms = [mtile() for _ in range(4)]
                for (mt, mr), (pb, tw) in zip(ms, ((bre, 0), (bim, 1), (bre, 1), (bim, 0))):
                    kb.op('dve', lambda e, mt=mt, pb=pb, tw=tw: e.tensor_tensor(out=mt[:, 0:N2], in0=psb[pb][:, 0:N2], in1=TWI[:, tw, :], op=ALU.mult),
                          reads=[f'ps{pb}', 'TWI'], writes=wr([mr]))
                kb.op('pool', lambda e, E=E, c=c, ms=ms: e.tensor_tensor(out=E[:, 0, c, :], in0=ms[0][0][:, 0:N2], in1=ms[1][0][:, 0:N2], op=ALU.subtract),
                      reads=[ms[0][1], ms[1][1]], writes=wr([eres]))
                kb.op('pool', lambda e, E=E, c=c, ms=ms: e.tensor_tensor(out=E[:, 1, c, :], in0=ms[2][0][:, 0:N2], in1=ms[3][0][:, 0:N2], op=ALU.add),
                      reads=[ms[2][1], ms[3][1]], writes=[eres])
                release(bre, bim)
                yield

        def h3_D(ctx):
            o, ct, g = ctx['key']
            rows = ctx['rows']
            E, eres = ctx['E']
            X, xres = ctx['X']
            zi = nxt('zo', 2)
            ZO = zo[zi]; zores = f'zo{zi}'
            for c in range(4):
                b = bank()
                for ri in range(2):
                    kb.op('pe', lambda e, b=b, ri=ri, c=c, E=E: e.matmul(psb[b][0:NR, 0:N2], lhsT=G1[:, ri, :], rhs=E[:, ri, c, :], start=(ri == 0), stop=(ri == 1)),
                          reads=['G1', eres], writes=[f'ps{b}'])
                kb.op('dve', lambda e, b=b, c=c, ZO=ZO, X=X: e.tensor_tensor(out=ZO[:, c, :], in0=psb[b][0:NR, 0:N2], in1=X[:, c, :], op=ALU.mult),
                      reads=[f'ps{b}', xres], writes=wr([zores]))
                yield
            if o == 0:
                kb.dma('act', zc_s[1][rows, :].rearrange("c (a b) -> a c b", b=N2), ZO[:], reads=[zores], writes=['zc1_s'], is_output=False)
                if debug:
                    kb.dma('act', zdbg[rows, :].rearrange("c (a b) -> a c b", b=N2), ZO[:], reads=[zores], is_output=True)
            else:
                nfr = L // N2
                if nfr:
                    kb.dma('act', z2_d[rows, 0:nfr * N2].rearrange("c (a b) -> a c b", b=N2), ZO[0:nfr, :, :], reads=[zores], is_output=True)
                if L % N2:
                    kb.dma('act', z2_d[rows, nfr * N2:L].rearrange("(a c) b -> a c b", a=1), ZO[nfr:nfr + 1, :, 0:L % N2], reads=[zores], is_output=True)

        for o in range(2):
            keys = [(o, ct, g) for ct in range(2) for g in range(NG4)]
            ctxs = [dict(key=k, rows=slice(k[1] * 128 + k[2] * 4, k[1] * 128 + k[2] * 4 + 4)) for k in keys]
            n = len(keys)
            for i in range(n + 3):
                gens = []
                if i < n:
                    gens.append(h3_A(ctxs[i]))
                if 0 <= i - 1 < n:
                    gens.append(h3_B(ctxs[i - 1]))
                if 0 <= i - 2 < n:
                    gens.append(h3_C(ctxs[i - 2]))
                if 0 <= i - 3 < n:
                    gens.append(h3_D(ctxs[i - 3]))
                run_interleaved(gens)


def layout_hT(hT, TG=256):
    D, L = hT.shape
    KD = D // 128
    NGRP = (L + TG - 1) // TG
    p = np.zeros((D, NGRP * TG), hT.dtype)
    p[:, :L] = hT
    p = p.reshape(KD, 128, NGRP, TG).transpose(2, 1, 0, 3)
    return np.ascontiguousarray(p).reshape(NGRP, 128, KD * TG)


def build_A(D, L, TG=256, eps=1e-6, hyena=None, debug=False):
    KD = D // 128
    HD = 128
    NQ = 2
    NHY = 6
    NCOL = NQ * HD + 2 * HD + NHY * 128
    nc = bass.Bass("TRN2", target_bir_lowering=False)
    dr = lambda n, s, dt=F32, kind="ExternalInput": nc.dram_tensor(n, list(s), dt, kind=kind).ap()
    NGRP = (L + TG - 1) // TG
    hT_d = dr("hT", [NGRP, 128, KD * TG])
    w_d = dr("w", [D, NCOL])
    gm_d = dr("gmix", [128, KD])
    qk_d = dr("qkn", [128, 2])
    cos_d = dr("cosT", [128, L])
    sin_d = dr("sinT", [128, L])
    rT_d = dr("rT", [128, 128])
    ya_d = dr("yaT", [NQ * HD, L], BF16, "ExternalOutput")
    qT_s = dr("qT_s", [NQ, HD, L], BF16, "Internal")
    kT_s = dr("kT_s", [HD, L], BF16, "Internal")
    v_s = dr("v_s", [L, HD], BF16, "Internal")
    hy_s = dr("hy_s", [NHY * 128, L + 2], F32, "ExternalOutput" if debug else "Internal")
    if debug:
        qdbg = dr("qdbg", [NQ, HD, L], BF16, "ExternalOutput")
        kdbg = dr("kdbg", [HD, L], BF16, "ExternalOutput")
        vdbg = dr("vdbg", [L, HD], BF16, "ExternalOutput")

    groups = [(t0, min(TG, L - t0)) for t0 in range(0, L, TG)]
    with ExitStack() as stack:
        kb = KB(nc, stack)
        psb = [kb.ps([128, 512]) for _ in range(8)]
        ones = kb.sb([128, 128], F32)
        onesb = kb.sb([128, 128], BF16)
        epst = kb.sb([128, 1], F32)
        zer = kb.sb([128, 8], F32)
        kb.op('pool', lambda e: e.memset(ones[:], 1.0), writes=['ones'])
        kb.op('pool', lambda e: e.memset(onesb[:], 1.0), writes=['onesb'])
        kb.op('pool', lambda e: e.memset(epst[:], eps), writes=['eps'])
        kb.op('pool', lambda e: e.memset(zer[:], 0.0), writes=['zer'])
        for r in range(NHY):
            kb.dma('sp', hy_s[r * 128:(r + 1) * 128, 0:1], zer[:, 0:1], reads=['zer'], writes=['hy_s'], allow_slow_non_contiguous=True)
            kb.dma('sp', hy_s[r * 128:(r + 1) * 128, L + 1:L + 2], zer[:, 0:1], reads=['zer'], writes=['hy_s'], allow_slow_non_contiguous=True)

        kb.mark('A1')
        with ExitStack() as st1:
            sb1 = lambda shape, dt: st1.enter_context(nc.sbuf_tensor(f"a1_{kb.nbuf}_{(kb.__setattr__('nbuf', kb.nbuf + 1))}", list(shape), dt))
            W = sb1([128, KD, NCOL], BF16)
            gm = sb1([128, KD], F32)
            qkn = sb1([128, 2], F32)
            rT = sb1([128, 128], F32)
            hT = [sb1([128, KD, TG], F32) for _ in range(2)]
            uTs = [sb1([128, KD, TG], BF16) for _ in range(2)]
            sqs = [sb1([128, 4, TG], F32) for _ in range(2)]
            acc = sb1([128, TG], F32); accp = sb1([128, TG], F32); rstd = sb1([128, TG], F32)
            cosbs = [sb1([128, TG], F32) for _ in range(2)]; sinbs = [sb1([128, TG], F32) for _ in range(2)]
            xs = [sb1([128, TG], F32) for _ in range(3)]
            x2 = [sb1([128, TG], F32) for _ in range(3)]
            rs2 = [sb1([128, TG], F32) for _ in range(3)]
            qo = [sb1([128, TG], BF16) for _ in range(3)]
            hyb = [sb1([128, TG], F32) for _ in range(3)]
            vb = [sb1([128, HD], BF16) for _ in range(2)]
            kb.dma('sp', gm[:], gm_d[:, :], writes=['gm'])
            kb.dma('sp', qkn[:], qk_d[:, :], writes=['qkn'])
            kb.dma('sp', rT[:], rT_d[:, :], writes=['rT'])
            for k0 in range(0, KD, 8):
                nk = min(8, KD - k0)
                for c0 in range(0, NCOL, 512):
                    cc = min(512, NCOL - c0)
                    kb.dma('pool', W[:, k0:k0 + nk, c0:c0 + cc],
                           w_d[k0 * 128:(k0 + nk) * 128, c0:c0 + cc].rearrange("(k p) n -> p k n", p=128), writes=['W'])
            def stage_N(gi, t0, tg):
                hb = hT[gi % 2]; hres = f'hT{gi % 2}'
                uT = uTs[gi % 2]; ures = f'uT{gi % 2}'
                cosb = cosbs[gi % 2]; sinb = sinbs[gi % 2]; cres = f'cosb{gi % 2}'; sres_ = f'sinb{gi % 2}'
                kb.dma('sp', hb[:, :, 0:tg], hT_d[gi].rearrange("p (k t) -> p k t", t=TG)[:, :, 0:tg], writes=[hres])
                kb.dma('sp', cosb[:, 0:tg], cos_d[:, t0:t0 + tg], writes=[cres])
                kb.dma('sp', sinb[:, 0:tg], sin_d[:, t0:t0 + tg], writes=[sres_])
                for pi_, p0 in enumerate(range(0, KD, 4)):
                    npc = min(4, KD - p0)
                    sq = sqs[pi_ % 2]; sqres = f'sq{pi_ % 2}'
                    kb.op('act', lambda e, p0=p0, npc=npc, hb=hb, tg=tg, sq=sq: e.activation(out=sq[:, 0:npc, 0:tg], in_=hb[:, p0:p0 + npc, 0:tg], func=AF.Square),
                          reads=[hres], writes=[sqres])
                    for c in range(npc):
                        if p0 == 0 and c == 0:
                            kb.op('pool', lambda e, sq=sq, tg=tg: e.tensor_copy(out=acc[:, 0:tg], in_=sq[:, 0, 0:tg]), reads=[sqres], writes=['acc'])
                        else:
                            kb.op('pool', lambda e, sq=sq, c=c, tg=tg: e.tensor_tensor(out=acc[:, 0:tg], in0=acc[:, 0:tg], in1=sq[:, c, 0:tg], op=ALU.add),
                                  reads=[sqres, 'acc'], writes=['acc'])
                kb.op('pe', lambda e, tg=tg: e.matmul(psb[7][:, 0:tg], lhsT=ones[:], rhs=acc[:, 0:tg], start=True, stop=True),
                      reads=['ones', 'acc'], writes=['ps7'])
                kb.op('act', lambda e, tg=tg: e.activation(out=rstd[:, 0:tg], in_=psb[7][:, 0:tg], func=AF.Sqrt, bias=epst[:], scale=1.0 / D),
                      reads=['ps7', 'eps'], writes=['rstd'])
                kb.op('dve', lambda e, tg=tg: e.reciprocal(out=rstd[:, 0:tg], in_=rstd[:, 0:tg]), reads=['rstd'], writes=['rstd'])
                for k in range(KD):
                    kb.op('dve', lambda e, k=k, hb=hb, tg=tg, uT=uT: e.scalar_tensor_tensor(out=uT[:, k, 0:tg], in0=hb[:, k, 0:tg], scalar=gm[:, k:k + 1],
                                                                                   in1=rstd[:, 0:tg], op0=ALU.mult, op1=ALU.mult),
                          reads=[hres, 'gm', 'rstd'], writes=[ures])

            def stage_M(gi, t0, tg):
                hb = hT[gi % 2]; hres = f'hT{gi % 2}'
                uT = uTs[gi % 2]; ures = f'uT{gi % 2}'
                cosb = cosbs[gi % 2]; sinb = sinbs[gi % 2]; cres = f'cosb{gi % 2}'; sres_ = f'sinb{gi % 2}'
                for k in range(KD):
                    for j in range(3):
                        kb.op('pe', lambda e, j=j, k=k, tg=tg, uT=uT: e.matmul(psb[j][:, 0:tg], lhsT=W[:, k, j * 128:(j + 1) * 128], rhs=uT[:, k, 0:tg],
                                                                       start=(k == 0), stop=(k == KD - 1)),
                              reads=['W', ures], writes=[f'ps{j}'])
                for j in range(3):
                    kb.op('act', lambda e, j=j, tg=tg: e.copy(out=xs[j][:, 0:tg], in_=psb[j][:, 0:tg]), reads=[f'ps{j}'], writes=[f'xs{j}'])
                    kb.op('act', lambda e, j=j, tg=tg: e.activation(out=x2[j][:, 0:tg], in_=psb[j][:, 0:tg], func=AF.Square), reads=[f'ps{j}'], writes=[f'x2{j}'])

                def hy_pass(js):
                    for k in range(KD):
                        for j in js:
                            b = 3 + (j % 3)
                            kb.op('pe', lambda e, j=j, k=k, b=b, tg=tg, uT=uT: e.matmul(psb[b][:, 0:tg], lhsT=W[:, k, (4 + j) * 128:(5 + j) * 128], rhs=uT[:, k, 0:tg],
                                                                                start=(k == 0), stop=(k == KD - 1)),
                                  reads=['W', ures], writes=[f'ps{b}'])
                    for j in js:
                        b = 3 + (j % 3)
                        kb.op('act' if j % 2 == 0 else 'dve',
                              (lambda e, j=j, b=b, tg=tg: e.copy(out=hyb[j % 3][:, 0:tg], in_=psb[b][:, 0:tg])) if j % 2 == 0 else
                              (lambda e, j=j, b=b, tg=tg: e.tensor_copy(out=hyb[j % 3][:, 0:tg], in_=psb[b][:, 0:tg])),
                              reads=[f'ps{b}'], writes=[f'hyb{j % 3}'])
                        kb.dma('sp', hy_s[j * 128:(j + 1) * 128, 1 + t0:1 + t0 + tg], hyb[j % 3][:, 0:tg], reads=[f'hyb{j % 3}'], writes=['hy_s'],
                               is_output=debug)

                hy_pass((0, 1, 2))
                for s0 in range(0, tg, 128):
                    sn = min(128, tg - s0)
                    vi = (s0 // 128) % 2
                    for k in range(KD):
                        kb.op('pe', lambda e, k=k, s0=s0, sn=sn, uT=uT: e.matmul(psb[6][0:sn, 0:HD], lhsT=uT[:, k, s0:s0 + sn], rhs=W[:, k, 3 * 128:4 * 128],
                                                                         start=(k == 0), stop=(k == KD - 1)),
                              reads=['W', ures], writes=['ps6'])
                    kb.op('act', lambda e, vi=vi, sn=sn: e.copy(out=vb[vi][0:sn, :], in_=psb[6][0:sn, 0:HD]), reads=['ps6'], writes=[f'vb{vi}'])
                    kb.dma('sp', v_s[t0 + s0:t0 + s0 + sn, :], vb[vi][0:sn, :], reads=[f'vb{vi}'], writes=['v_s'])
                    if debug:
                        kb.dma('sp', vdbg[t0 + s0:t0 + s0 + sn, :], vb[vi][0:sn, :], reads=[f'vb{vi}'], is_output=True)
                hy_pass((3, 4, 5))
                for j in range(3):
                    kb.op('pe', lambda e, j=j, tg=tg: e.matmul(psb[j][:, 0:tg], lhsT=ones[:], rhs=x2[j][:, 0:tg], start=True, stop=True),
                          reads=['ones', f'x2{j}'], writes=[f'ps{j}'])
                for j in range(3):
                    gcol = 0 if j < 2 else 1
                    kb.op('act', lambda e, j=j, tg=tg: e.activation(out=rs2[j][:, 0:tg], in_=psb[j][:, 0:tg], func=AF.Sqrt, bias=epst[:], scale=1.0 / HD),
                          reads=[f'ps{j}', 'eps'], writes=[f'rs2{j}'])
                    kb.op('dve', lambda e, j=j, tg=tg: e.reciprocal(out=rs2[j][:, 0:tg], in_=rs2[j][:, 0:tg]), reads=[f'rs2{j}'], writes=[f'rs2{j}'])
                    kb.op('dve', lambda e, j=j, tg=tg, gcol=gcol: e.scalar_tensor_tensor(out=xs[j][:, 0:tg], in0=xs[j][:, 0:tg], scalar=qkn[:, gcol:gcol + 1],
                                                                                       in1=rs2[j][:, 0:tg], op0=ALU.mult, op1=ALU.mult),
                          reads=[f'xs{j}', f'rs2{j}', 'qkn'], writes=[f'xs{j}'])
                for j in range(3):
                    kb.op('pe', lambda e, j=j, tg=tg: e.matmul(psb[j][:, 0:tg], lhsT=rT[:], rhs=xs[j][:, 0:tg], start=True, stop=True),
                          reads=['rT', f'xs{j}'], writes=[f'ps{j}'])
                for j in range(3):
                    kb.op('dve', lambda e, j=j, tg=tg, sinb=sinb: e.tensor_tensor(out=x2[j][:, 0:tg], in0=psb[j][:, 0:tg], in1=sinb[:, 0:tg], op=ALU.mult),
                          reads=[f'ps{j}', sres_], writes=[f'x2{j}'])
                    kb.op('dve', lambda e, j=j, tg=tg, cosb=cosb: e.tensor_tensor(out=xs[j][:, 0:tg], in0=xs[j][:, 0:tg], in1=cosb[:, 0:tg], op=ALU.mult),
                          reads=[f'xs{j}', cres], writes=[f'xs{j}'])
                    kb.op('dve', lambda e, j=j, tg=tg: e.tensor_tensor(out=qo[j][:, 0:tg], in0=xs[j][:, 0:tg], in1=x2[j][:, 0:tg], op=ALU.add),
                          reads=[f'xs{j}', f'x2{j}'], writes=[f'qo{j}'])
                    dst = qT_s[j, :, t0:t0 + tg] if j < 2 else kT_s[:, t0:t0 + tg]
                    kb.dma('sp', dst, qo[j][:, 0:tg], reads=[f'qo{j}'], writes=['qk_s'])
                    if debug:
                        dst = qdbg[j, :, t0:t0 + tg] if j < 2 else kdbg[:, t0:t0 + tg]
                        kb.dma('sp', dst, qo[j][:, 0:tg], reads=[f'qo{j}'], is_output=True)

            stage_N(0, *groups[0])
            for gi, (t0, tg) in enumerate(groups):
                if gi + 1 < len(groups):
                    stage_N(gi + 1, *groups[gi + 1])
                stage_M(gi, t0, tg)

        kb.mark('ATT')
        NKT = (L + 127) // 128
        QB = 512
        scale = HD ** -0.5
        with ExitStack() as st2:
            sb2 = lambda shape, dt: st2.enter_context(nc.sbuf_tensor(f"a2_{kb.nbuf}_{(kb.__setattr__('nbuf', kb.nbuf + 1))}", list(shape), dt))
            KT = sb2([128, L], BF16)
            QT = sb2([128, NQ, L], BF16)
            V = sb2([128, NKT, HD], BF16)
            NS = 4
            LOOK = 3
            pT = [sb2([128, QB], BF16) for _ in range(NS)]
            rc = sb2([128, QB], F32)
            accD = sb2([128, QB], F32); accP = sb2([128, QB], F32)
            ob = [sb2([128, QB], BF16) for _ in range(2)]
            a1res = ['W', 'uT0', 'uT1', 'hT0', 'hT1', 'sq0', 'sq1', 'acc', 'accp', 'rstd', 'cosb0', 'cosb1', 'sinb0', 'sinb1', 'gm', 'qkn', 'rT'] + \
                    [f'{n}{j}' for n in ('xs', 'x2', 'rs2', 'qo') for j in range(3)] + [f'hyb{j}' for j in range(3)] + ['vb0', 'vb1']
            kb.dma('sp', KT[:], kT_s[:, :], reads=['qk_s'], writes=['KT'] + a1res)
            for h in range(NQ):
                kb.dma('sp', QT[:, h, :], qT_s[h], reads=['qk_s'], writes=['QT'] + a1res)
            nfull = L // 128
            for kt0 in range(0, nfull, 16):
                kn_ = min(16, nfull - kt0)
                kb.dma('sp', V[:, kt0:kt0 + kn_, :], v_s[kt0 * 128:(kt0 + kn_) * 128, :].rearrange("(kt p) d -> p kt d", p=128), reads=['v_s'], writes=['V'] + a1res)
            if L % 128:
                kb.dma('sp', V[0:L % 128, nfull, :], v_s[nfull * 128:L, :], reads=['v_s'], writes=['V'] + a1res)
            it = 0
            blk = 0
            for h in range(NQ):
                for bi, q0 in enumerate(range(0, L, QB)):
                    qn = min(QB, L - q0)
                    ob_i = blk % 2
                    po, pd = (4, 5) if (blk % 2 == 0) else (6, 7)
                    blk += 1
                    kb.op('dve', lambda e, qn=qn: e.memset(accD[:, 0:qn], 0.0), writes=['accD'] + a1res)
                    kb.op('pool', lambda e, qn=qn: e.memset(accP[:, 0:qn], 0.0), writes=['accP'] + a1res)

                    def s_stage(kt, h=h, q0=q0, qn=qn):
                        kn = min(128, L - kt * 128)
                        sbk = kt % NS
                        kb.op('pe', lambda e: e.matmul(
                            psb[sbk][0:kn, 0:qn], lhsT=KT[:, kt * 128:kt * 128 + kn], rhs=QT[:, h, q0:q0 + qn], start=True, stop=True),
                            reads=['KT', 'QT'], writes=[f'ps{sbk}'])
                        kb.op('act', lambda e: e.activation(out=pT[sbk][0:kn, 0:qn], in_=psb[sbk][0:kn, 0:qn], func=AF.Exp, scale=scale),
                              reads=[f'ps{sbk}'], writes=[f'pT{sbk}'])

                    def pv_stage(kt, qn=qn, po=po):
                        kn = min(128, L - kt * 128)
                        pi = kt % NS
                        kb.op('pe', lambda e: e.matmul(
                            psb[po][:, 0:qn], lhsT=V[0:kn, kt, :], rhs=pT[pi][0:kn, 0:qn], start=(kt == 0), stop=(kt == NKT - 1)),
                            reads=['V', f'pT{pi}'], writes=[f'ps{po}'])
                        if kt % 2 == 0:
                            kb.op('dve', lambda e: e.tensor_tensor(out=accD[0:kn, 0:qn], in0=accD[0:kn, 0:qn], in1=pT[pi][0:kn, 0:qn], op=ALU.add),
                                  reads=['accD', f'pT{pi}'], writes=['accD'])
                        else:
                            kb.op('pool', lambda e: e.tensor_tensor(out=accP[0:kn, 0:qn], in0=accP[0:kn, 0:qn], in1=pT[pi][0:kn, 0:qn], op=ALU.add),
                                  reads=['accP', f'pT{pi}'], writes=['accP'])

                    for i in range(NKT + LOOK):
                        if i < NKT:
                            s_stage(i)
                        if i - LOOK >= 0:
                            pv_stage(i - LOOK)
                    kb.op('pe', lambda e, qn=qn, pd=pd: e.matmul(psb[pd][:, 0:qn], lhsT=ones[:], rhs=accD[:, 0:qn], start=True, stop=False),
                          reads=['ones', 'accD'], writes=[f'ps{pd}'])
                    kb.op('pe', lambda e, qn=qn, pd=pd: e.matmul(psb[pd][:, 0:qn], lhsT=ones[:], rhs=accP[:, 0:qn], start=False, stop=True),
                          reads=['ones', 'accP'], writes=[f'ps{pd}'])
                    kb.op('dve', lambda e, qn=qn, pd=pd: e.reciprocal(out=rc[:, 0:qn], in_=psb[pd][:, 0:qn]), reads=[f'ps{pd}'], writes=['rc'])
                    kb.op('dve', lambda e, qn=qn, po=po, ob_i=ob_i: e.tensor_tensor(out=ob[ob_i][:, 0:qn], in0=psb[po][:, 0:qn], in1=rc[:, 0:qn], op=ALU.mult),
                          reads=[f'ps{po}', 'rc'], writes=[f'ob{ob_i}'])
                    kb.dma('act', ya_d[h * HD:(h + 1) * HD, q0:q0 + qn], ob[ob_i][:, 0:qn], reads=[f'ob{ob_i}'], is_output=True)
        if hyena:
            hyena_phase(nc, kb, psb, hy_s, dr, L, debug)
        kb.mark('END')
        nc._kb_marks = kb.marks
        kb.emit()
    return nc


def build_B(D, AW, DFF, NG, TI, eps=1e-6):
    T = TI + 2
    KD, KA, KF = D // 128, AW // 128, DFF // 128
    nc = bass.Bass("TRN2", target_bir_lowering=False)
    dr = lambda n, s, dt=F32, kind="ExternalInput": nc.dram_tensor(n, list(s), dt, kind=kind).ap()
    hT_d = dr("hT", [NG, D, T])
    ya_d = dr("yaT", [NG, AW, T], BF16)
    zb_d = dr("zbT", [NG, AW, T], BF16)
    wg_d = dr("wg", [D, 2 * D])
    wa_d = dr("wa", [AW, D])
    wb_d = dr("wb", [AW, D])
    wo_d = dr("wo", [D, D])
    wfg_d = dr("wfg", [D, DFF])
    wfu_d = dr("wfu", [D, DFF])
    wfd_d = dr("wfd", [DFF, D])
    gm_d = dr("gmix", [128, KD])
    gf_d = dr("gffn", [128, KD])
    gl_d = dr("gfin", [128, KD])
    cw_d = dr("cw", [128, KF * 3])
    cb_d = dr("cb", [128, KF])
    out_d = dr("outT", [NG, D, TI], F32, "ExternalOutput")

    with ExitStack() as stack:
        kb = KB(nc, stack)
        hT = kb.sb([128, KD, T], F32)
        uT = kb.sb([128, KD, T], BF16)
        NBIG = max((KD + 2 * KA) * T, KF * T)
        big = kb.sb([128, NBIG], BF16)
        mixT = big[:, 0:KD * T].rearrange("p (k t) -> p k t", t=T)
        yaT = big[:, KD * T:(KD + KA) * T].rearrange("p (k t) -> p k t", t=T)
        zbT = big[:, (KD + KA) * T:(KD + 2 * KA) * T].rearrange("p (k t) -> p k t", t=T)
        actT = big[:, 0:KF * T].rearrange("p (k t) -> p k t", t=T)
        NKW = 8
        NW = 4
        wbuf = [kb.sb([128, NKW, 512], BF16) for _ in range(NW)]
        gm = kb.sb([128, KD], F32); gf = kb.sb([128, KD], F32); gl = kb.sb([128, KD], F32)
        cw = kb.sb([128, KF * 3], F32); cb = kb.sb([128, KF], F32)
        ones = kb.sb([128, 128], F32)
        epst = kb.sb([128, 1], F32)
        sq = kb.sb([128, 4, T], F32)
        acc = kb.sb([128, T], F32); accp = kb.sb([128, T], F32)
        rstd = kb.sb([128, T], F32)
        tA = [kb.sb([128, T], F32) for _ in range(4)]
        tB = [kb.sb([128, T], F32) for _ in range(4)]
        psb = [kb.ps([128, 512]) for _ in range(8)]

        for (t_, d_, n_) in ((gm, gm_d, 'gm'), (gf, gf_d, 'gf'), (gl, gl_d, 'gl'), (cw, cw_d, 'cw'), (cb, cb_d, 'cb')):
            kb.dma('sp', t_[:], d_[:, :], writes=[n_])
        kb.op('pool', lambda e: e.memset(ones[:], 1.0), writes=['ones'])
        kb.op('pool', lambda e: e.memset(epst[:], eps), writes=['eps'])

        wctr = [0]

        def load_w(W, k0, nk, n0, ncols):
            i = wctr[0] % NW
            wctr[0] += 1
            src = W[k0 * 128:(k0 + nk) * 128, n0:n0 + ncols].rearrange("(k p) n -> p k n", p=128)
            kb.dma('pool', wbuf[i][:, 0:nk, 0:ncols], src, writes=[f'w{i}'])
            return wbuf[i], f'w{i}'

        pctr = [0]

        def gemmT(W, KC, n0, ncols, rhs_fn, rhs_res, epi):
            ntl = ncols // 128
            base = (pctr[0] % 2) * 4
            pctr[0] += 1
            for k0 in range(0, KC, NKW):
                nk = min(NKW, KC - k0)
                wt, wres = load_w(W, k0, nk, n0, ncols)
                for kk in range(nk):
                    k = k0 + kk
                    for nt in range(ntl):
                        kb.op('pe', lambda e, wt=wt, kk=kk, nt=nt, k=k, b=base + nt: e.matmul(
                            psb[b][:, 0:T], lhsT=wt[:, kk, nt * 128:(nt + 1) * 128], rhs=rhs_fn(k),
                            start=(k == 0), stop=(k == KC - 1)),
                            reads=[wres] + rhs_res, writes=[f'ps{base + nt}'])
            for nt in range(ntl):
                epi(nt, psb[base + nt], f'ps{base + nt}')

        def rmsnorm(gain, gres, src_res, dst_res):
            for p0 in range(0, KD, 4):
                npc = min(4, KD - p0)
                kb.op('act', lambda e, p0=p0, npc=npc: e.activation(out=sq[:, 0:npc, :], in_=hT[:, p0:p0 + npc, :], func=AF.Square),
                      reads=[src_res], writes=['sq'])
                dst = acc if p0 == 0 else accp
                kb.op('dve', lambda e, npc=npc, dst=dst: e.tensor_reduce(
                    out=dst[:], in_=sq[:, 0:npc, :].rearrange("p k t -> p t k"), axis=AX.X, op=ALU.add),
                    reads=['sq'], writes=['acc' if p0 == 0 else 'accp'])
                if p0 > 0:
                    kb.op('dve', lambda e: e.tensor_tensor(out=acc[:], in0=acc[:], in1=accp[:], op=ALU.add),
                          reads=['acc', 'accp'], writes=['acc'])
            kb.op('pe', lambda e: e.matmul(psb[0][:, 0:T], lhsT=ones[:], rhs=acc[:], start=True, stop=True),
                  reads=['ones', 'acc'], writes=['ps0'])
            kb.op('act', lambda e: e.activation(out=rstd[:], in_=psb[0][:, 0:T], func=AF.Sqrt, bias=epst[:], scale=1.0 / D),
                  reads=['ps0', 'eps'], writes=['rstd'])
            kb.op('dve', lambda e: e.reciprocal(out=rstd[:], in_=rstd[:]), reads=['rstd'], writes=['rstd'])

        for g in range(NG):
            kb.dma('sp', hT[:], hT_d[g].rearrange("(k p) t -> p k t", p=128), writes=['hT'])
            kb.dma('sp', yaT, ya_d[g].rearrange("(k p) t -> p k t", p=128), writes=['yaT', 'actT'])
            kb.dma('sp', zbT, zb_d[g].rearrange("(k p) t -> p k t", p=128), writes=['zbT', 'actT'])
            rmsnorm(gm, 'gm', 'hT', 'uT')
            for k in range(KD):
                kb.op('dve', lambda e, k=k: e.scalar_tensor_tensor(out=uT[:, k, :], in0=hT[:, k, :], scalar=gm[:, k:k + 1],
                                                                   in1=rstd[:], op0=ALU.mult, op1=ALU.mult),
                      reads=['hT', 'gm', 'rstd'], writes=['uT'])
            for n0 in range(0, D, 512):
                ncols = min(512, D - n0)

                def epi_sig(tl):
                    def f(nt, ps, pres):
                        kb.op('act', lambda e: e.activation(out=tl[nt][:], in_=ps[:, 0:T], func=AF.Sigmoid),
                              reads=[pres], writes=[f'{id(tl)}_{nt}'])
                    return f

                def epi_mulA(nt, ps, pres):
                    kb.op('dve', lambda e: e.tensor_tensor(out=tA[nt][:], in0=ps[:, 0:T], in1=tA[nt][:], op=ALU.mult),
                          reads=[pres, f'{id(tA)}_{nt}'], writes=[f'{id(tA)}_{nt}'])

                def epi_mulB(nt, ps, pres, n0=n0):
                    kb.op('dve', lambda e: e.tensor_tensor(out=tB[nt][:], in0=ps[:, 0:T], in1=tB[nt][:], op=ALU.mult),
                          reads=[pres, f'{id(tB)}_{nt}'], writes=[f'{id(tB)}_{nt}'])
                    kb.op('dve', lambda e: e.tensor_tensor(out=mixT[:, n0 // 128 + nt, :], in0=tA[nt][:], in1=tB[nt][:], op=ALU.add),
                          reads=[f'{id(tA)}_{nt}', f'{id(tB)}_{nt}'], writes=['mixT', 'actT'])

                gemmT(wg_d, KD, n0, ncols, lambda k: uT[:, k, :], ['uT'], epi_sig(tA))
                gemmT(wa_d, KA, n0, ncols, lambda k: yaT[:, k, :], ['yaT'], epi_mulA)
                gemmT(wg_d, KD, D + n0, ncols, lambda k: uT[:, k, :], ['uT'], epi_sig(tB))
                gemmT(wb_d, KA, n0, ncols, lambda k: zbT[:, k, :], ['zbT'], epi_mulB)
            for n0 in range(0, D, 512):
                ncols = min(512, D - n0)

                def epi_res(nt, ps, pres, n0=n0):
                    kb.op('dve', lambda e: e.tensor_tensor(out=hT[:, n0 // 128 + nt, :], in0=ps[:, 0:T], in1=hT[:, n0 // 128 + nt, :], op=ALU.add),
                          reads=[pres, 'hT'], writes=['hT'])
                gemmT(wo_d, KD, n0, ncols, lambda k: mixT[:, k, :], ['mixT'], epi_res)
            rmsnorm(gf, 'gf', 'hT', 'uT')
            for k in range(KD):
                kb.op('dve', lambda e, k=k: e.scalar_tensor_tensor(out=uT[:, k, :], in0=hT[:, k, :], scalar=gf[:, k:k + 1],
                                                                   in1=rstd[:], op0=ALU.mult, op1=ALU.mult),
                      reads=['hT', 'gf', 'rstd'], writes=['uT'])
            for n0 in range(0, DFF, 512):
                ncols = min(512, DFF - n0)

                def epi_gate(nt, ps, pres, n0=n0):
                    f = n0 // 128 + nt
                    kb.op('act', lambda e: e.copy(out=tA[nt][:], in_=ps[:, 0:T]), reads=[pres], writes=[f'ga{nt}'])
                    kb.op('dve', lambda e: e.tensor_scalar(out=tB[nt][:, 1:T - 1], in0=tA[nt][:, 1:T - 1], scalar1=cw[:, 3 * f + 1:3 * f + 2],
                                                           scalar2=cb[:, f:f + 1], op0=ALU.mult, op1=ALU.add),
                          reads=[f'ga{nt}', 'cw', 'cb'], writes=[f'gb{nt}'])
                    kb.op('dve', lambda e: e.scalar_tensor_tensor(out=tB[nt][:, 1:T - 1], in0=tA[nt][:, 0:T - 2], scalar=cw[:, 3 * f:3 * f + 1],
                                                                  in1=tB[nt][:, 1:T - 1], op0=ALU.mult, op1=ALU.add),
                          reads=[f'ga{nt}', f'gb{nt}'], writes=[f'gb{nt}'])
                    kb.op('dve', lambda e: e.scalar_tensor_tensor(out=tB[nt][:, 1:T - 1], in0=tA[nt][:, 2:T], scalar=cw[:, 3 * f + 2:3 * f + 3],
                                                                  in1=tB[nt][:, 1:T - 1], op0=ALU.mult, op1=ALU.add),
                          reads=[f'ga{nt}', f'gb{nt}'], writes=[f'gb{nt}'])
                    kb.op('act', lambda e: e.activation(out=tB[nt][:, 1:T - 1], in_=tB[nt][:, 1:T - 1], func=AF.Silu),
                          reads=[f'gb{nt}'], writes=[f'gb{nt}'])

                def epi_up(nt, ps, pres, n0=n0):
                    f = n0 // 128 + nt
                    kb.op('dve', lambda e: e.tensor_tensor(out=actT[:, f, 1:T - 1], in0=ps[:, 1:T - 1], in1=tB[nt][:, 1:T - 1], op=ALU.mult),
                          reads=[pres, f'gb{nt}'], writes=['actT', 'mixT', 'yaT', 'zbT'])
                gemmT(wfg_d, KD, n0, ncols, lambda k: uT[:, k, :], ['uT'], epi_gate)
                gemmT(wfu_d, KD, n0, ncols, lambda k: uT[:, k, :], ['uT'], epi_up)
            for n0 in range(0, D, 512):
                ncols = min(512, D - n0)

                def epi_res2(nt, ps, pres, n0=n0):
                    kb.op('dve', lambda e: e.tensor_tensor(out=hT[:, n0 // 128 + nt, 1:T - 1], in0=ps[:, 1:T - 1],
                                                           in1=hT[:, n0 // 128 + nt, 1:T - 1], op=ALU.add),
                          reads=[pres, 'hT'], writes=['hT'])
                gemmT(wfd_d, KF, n0, ncols, lambda k: actT[:, k, :], ['actT'], epi_res2)
            rmsnorm(gl, 'gl', 'hT', 'hT')
            for k in range(KD):
                kb.op('dve', lambda e, k=k: e.scalar_tensor_tensor(out=hT[:, k, :], in0=hT[:, k, :], scalar=gl[:, k:k + 1],
                                                                   in1=rstd[:], op0=ALU.mult, op1=ALU.mult),
                      reads=['hT', 'gl', 'rstd'], writes=['hT'])
            kb.dma('sp', out_d[g].rearrange("(k p) t -> p k t", p=128), hT[:, :, 1:T - 1], reads=['hT'], is_output=True)
        kb.emit()
    return nc


D_MODEL = 4096; SEQ = 16384; N_META = 16; GRID_W = 64; D_FF = 11008
AW = 2048; KVW = 512
NCORES = 8


def _pk(v, k):
    return np.ascontiguousarray(np.asarray(v, np.float32).reshape(k, 128).T)


def _rope_tables(L):
    rows = SEQ // GRID_W
    row = np.repeat(np.arange(rows, dtype=np.float32), GRID_W)
    col = np.tile(np.arange(GRID_W, dtype=np.float32), rows)
    meta = np.zeros((N_META,), np.float32)
    row = np.concatenate([meta, row]); col = np.concatenate([meta, col])
    inv_freq = (np.float32(10000.0) ** (-np.arange(32, dtype=np.float32) * np.float32(2.0) / np.float32(64))).astype(np.float32)
    ar = (row[:, None] * inv_freq).astype(np.float32); ac = (col[:, None] * inv_freq).astype(np.float32)
    cosT = np.concatenate([np.cos(ar), np.cos(ar), np.cos(ac), np.cos(ac)], 1).T
    sinT = np.concatenate([np.sin(ar), np.sin(ar), np.sin(ac), np.sin(ac)], 1).T
    return np.ascontiguousarray(cosT, np.float32), np.ascontiguousarray(sinT, np.float32)


def _rT():
    R = np.zeros((128, 128), np.float32)
    for a in range(2):
        for m in range(32):
            R[64 * a + m, 64 * a + 32 + m] = -1.0
            R[64 * a + 32 + m, 64 * a + m] = 1.0
    return np.ascontiguousarray(R.T)


def kernel(x, meta_tokens, norm_mix, w_in, q_norm, k_norm, hyena_conv_w, hyena_conv_b,
           filt_w1, filt_b1, filt_w2, filt_b2, filt_w3, filt_b3, filt_w4, filt_freq,
           hyena_skip, w_attn_branch, w_hyena_branch, w_out, norm_ffn, w_ffn_gate,
           w_ffn_up, ffn_conv_w, ffn_conv_b, w_ffn_down, norm_final):
    f32 = lambda a: np.asarray(a, np.float32)
    L = SEQ + N_META
    D = D_MODEL
    h = np.concatenate([f32(meta_tokens), f32(x)[0]], 0)
    hT = np.ascontiguousarray(h.T)
    hTg = layout_hT(hT)
    w_in0 = f32(w_in)[0]
    cosT, sinT = _rope_tables(L)
    consts = hy_consts(L)
    rT = _rT()
    gmix = _pk(f32(norm_mix)[0], D // 128)
    qkn = np.ascontiguousarray(np.stack([f32(q_norm)[0], f32(k_norm)[0]], 1))
    fbf = np.ascontiguousarray(np.stack([f32(filt_b1)[0], f32(filt_b2)[0], f32(filt_b3)[0], f32(filt_freq)[0]], 1))
    hcw_full = f32(hyena_conv_w)[0]; hcb_full = f32(hyena_conv_b)[0]; w4 = f32(filt_w4)[0]; skp = f32(hyena_skip)[0]
    HYB = AW + 2 * KVW
    in_a = []
    for c in range(NCORES):
        kv = c // 2
        cols = [np.arange(256 * c, 256 * c + 256), np.arange(AW + 128 * kv, AW + 128 * kv + 128),
                np.arange(AW + KVW + 128 * kv, AW + KVW + 128 * kv + 128)]
        bases = []
        for sec in range(3):
            for ct in range(2):
                bases.append(sec * 2048 + 256 * c + 128 * ct)
                cols.append(HYB + bases[-1] + np.arange(128))
        cols = np.concatenate(cols)
        d = dict(consts)
        d["hT"] = hTg
        d["w"] = np.ascontiguousarray(w_in0[:, cols])
        d["gmix"] = gmix; d["qkn"] = qkn; d["cosT"] = cosT; d["sinT"] = sinT; d["rT"] = rT
        d["nabsd"] = hy_delta(c)
        d["fw1"] = f32(filt_w1)[0]; d["fw2"] = f32(filt_w2)[0]; d["fw3"] = f32(filt_w3)[0]
        w4c = [w4[:, o * 4096 + dd * 2048 + 256 * c + 128 * ct: o * 4096 + dd * 2048 + 256 * c + 128 * ct + 128]
               for o in range(2) for dd in range(2) for ct in range(2)]
        d["fw4"] = np.ascontiguousarray(np.concatenate(w4c, 1))
        d["fbf"] = fbf
        d["skip"] = np.ascontiguousarray(np.stack([skp[o, 256 * c + 128 * ct: 256 * c + 128 * ct + 128] for o in range(2) for ct in range(2)], 1))
        d["hcw"] = np.ascontiguousarray(np.concatenate([hcw_full[:, b:b + 128].T for b in bases], 1))
        d["hcb"] = np.ascontiguousarray(np.stack([hcb_full[b:b + 128] for b in bases], 1))
        in_a.append(d)
    nca = build_A(D, L, hyena=True)
    ra = run_bass_kernel_spmd(nca, in_a, core_ids=list(range(NCORES)))
    yaT = np.concatenate([np.asarray(ra.results[c]["yaT"]) for c in range(NCORES)], 0)
    zbT = np.concatenate([np.asarray(ra.results[c]["z2T"]) for c in range(NCORES)], 0)
    del in_a, ra

    NG, TI = 5, 410
    T = TI + 2
    TC = NG * TI
    assert TC * NCORES == L
    def _padcols(a):
        p = np.zeros((a.shape[0], a.shape[1] + 2), a.dtype)
        p[:, 1:-1] = a
        return p
    hTp = _padcols(hT); yap = _padcols(yaT); zbp = _padcols(zbT)
    KF = D_FF // 128
    cwf = f32(ffn_conv_w)[0]
    cw = np.ascontiguousarray(cwf.T.reshape(KF, 128, 3).transpose(1, 0, 2).reshape(128, KF * 3))
    common = dict(
        wg=np.ascontiguousarray(w_in0[:, HYB + 6144:]), wa=f32(w_attn_branch)[0], wb=f32(w_hyena_branch)[0], wo=f32(w_out)[0],
        wfg=f32(w_ffn_gate)[0], wfu=f32(w_ffn_up)[0], wfd=f32(w_ffn_down)[0],
        gmix=gmix, gffn=_pk(f32(norm_ffn)[0], D // 128), gfin=_pk(f32(norm_final), D // 128),
        cw=cw, cb=_pk(f32(ffn_conv_b)[0], KF))
    in_b = []
    for c in range(NCORES):
        d = dict(common)
        st = [TC * c + TI * j for j in range(NG)]
        d["hT"] = np.ascontiguousarray(np.stack([hTp[:, s:s + T] for s in st], 0))
        d["yaT"] = np.ascontiguousarray(np.stack([yap[:, s:s + T] for s in st], 0))
        d["zbT"] = np.ascontiguousarray(np.stack([zbp[:, s:s + T] for s in st], 0))
        in_b.append(d)
    ncb = build_B(D, AW, D_FF, NG, TI)
    rb = run_bass_kernel_spmd(ncb, in_b, core_ids=list(range(NCORES)))
    out = np.empty((L, D), np.float32)
    for c in range(NCORES):
        o = np.asarray(rb.results[c]["outT"])
        for j in range(NG):
            s = TC * c + TI * j
            out[s:s + TI] = o[j].T
    return np.ascontiguousarray(out[N_META:][None])
```

```python
import numpy as np, math
from contextlib import ExitStack
import concourse.bass as bass
import concourse.mybir as mybir
from concourse.bass_utils import run_bass_kernel_spmd
import ml_dtypes

F32 = mybir.dt.float32
BF16 = mybir.dt.bfloat16
AF = mybir.ActivationFunctionType
ALU = mybir.AluOpType
AX = mybir.AxisListType
NPBF = ml_dtypes.bfloat16


class KB:
    ENG = ['pe', 'act', 'dve', 'pool', 'sp']
    LIM = 30000
    RING = 6

    def __init__(self, nc, stack):
        self.nc = nc
        self.stack = stack
        self.ops = {e: [] for e in self.ENG}
        self.cnt = {e: 0 for e in self.ENG}
        self.cursem = {e: None for e in self.ENG}
        self.res = {}
        self.seen = {e: {} for e in self.ENG}
        self.ring = {q: [] for q in ('sp', 'act', 'pool')}
        self.ridx = {q: 0 for q in ('sp', 'act', 'pool')}
        self.nsem = 0
        self.nbuf = 0
        self.out_tokens = []
        self.marks = []
        self.npe = 0

    def newsem(self):
        self.nsem += 1
        return self.stack.enter_context(self.nc.semaphore(f"s{self.nsem}"))

    def sb(self, shape, dt, name=None):
        self.nbuf += 1
        return self.stack.enter_context(self.nc.sbuf_tensor(name or f"b{self.nbuf}", list(shape), dt))

    def ps(self, shape, dt=F32, name=None):
        self.nbuf += 1
        return self.stack.enter_context(self.nc.psum_tensor(name or f"p{self.nbuf}", list(shape), dt))

    def _deps(self, eng, reads, writes):
        need = {}

        def add(tok, kind):
            sem, val, peng = tok
            if peng == eng:
                if eng == 'pe' or kind != 'raw':
                    return
            k = id(sem)
            if self.seen[eng].get(k, 0) >= val:
                return
            if k not in need or need[k][1] < val:
                need[k] = (sem, val)

        for r in reads:
            st = self.res.get(r)
            if st:
                for t in st[0].values():
                    add(t, 'raw')
        for w in writes:
            st = self.res.get(w)
            if st:
                for t in st[0].values():
                    add(t, 'waw')
                for t in st[1].values():
                    add(t, 'war')
        for k, (sem, val) in need.items():
            self.seen[eng][k] = val
        return list(need.values())

    def _commit(self, tok, reads, writes):
        k = id(tok[0])
        for r in reads:
            st = self.res.setdefault(r, [{}, {}])
            if k not in st[1] or st[1][k][1] < tok[1]:
                st[1][k] = tok
        for w in writes:
            st = self.res.setdefault(w, [{}, {}])
            if k not in st[0] or st[0][k][1] < tok[1]:
                st[0][k] = tok

    def mark(self, label):
        self.marks.append((label, self.npe))

    def op(self, eng, fn, reads=(), writes=()):
        if eng == 'pe':
            self.npe += 1
        if self.cursem[eng] is None or self.cnt[eng] >= self.LIM:
            self.cursem[eng] = self.newsem()
            self.cnt[eng] = 0
        sem = self.cursem[eng]
        self.cnt[eng] += 1
        tok = (sem, self.cnt[eng], eng)
        waits = self._deps(eng, reads, writes)
        self.ops[eng].append(('op', waits, fn, sem))
        self._commit(tok, reads, writes)
        return tok

    def dma(self, q, out, in_, reads=(), writes=(), is_output=False, **kw):
        ring = self.ring[q]
        if len(ring) < self.RING:
            ring.append([self.newsem(), 0])
            slot = ring[-1]
        else:
            i = self.ridx[q] % self.RING
            self.ridx[q] += 1
            slot = ring[i]
            if slot[1] >= 1800:
                ring[i] = slot = [self.newsem(), 0]
        waits = self._deps(q, reads, writes)
        if slot[1] > 0:
            k = id(slot[0])
            v = 16 * slot[1]
            if self.seen[q].get(k, 0) < v:
                self.seen[q][k] = v
                waits.append((slot[0], v))
        slot[1] += 1
        tok = (slot[0], 16 * slot[1], 'dma')
        self.ops[q].append(('dma', waits, out, in_, slot[0], kw))
        self._commit(tok, reads, writes)
        if is_output:
            self.out_tokens.append(tok)
        return tok

    def emit(self):
        fin = {}
        for sem, val, _ in self.out_tokens:
            k = id(sem)
            if k not in fin or fin[k][1] < val:
                fin[k] = (sem, val)
        self.ops['sp'].append(('wait', list(fin.values())))
        block = self.stack.enter_context(self.nc.Block())
        for eng, deco in (('pe', block.tensor), ('act', block.scalar), ('dve', block.vector),
                          ('pool', block.gpsimd), ('sp', block.sync)):
            ops = self.ops[eng]

            def body(e, ops=ops):
                for item in ops:
                    for sem, val in item[1]:
                        e.wait_ge(sem, val)
                    if item[0] == 'op':
                        item[2](e).then_inc(item[3], 1)
                    elif item[0] == 'dma':
                        e.dma_start(out=item[2], in_=item[3], **item[5]).then_inc(item[4], 16)
            deco(body)


N1, N2 = 128, 384
NFFT = N1 * N2
MIN_DECAY = math.log(1e-2) / 1.5
MAX_DECAY = math.log(1e-2) / 0.3


def hy_consts(L):
    NR = (L + N2 - 1) // N2
    c = {}
    n1 = np.arange(NR)[:, None]; k1 = np.arange(N1)[None, :]
    a = 2 * np.pi * n1 * k1 / N1
    c["F1"] = np.concatenate([np.cos(a), -np.sin(a)], 1).astype(NPBF)
    n2 = (np.arange(3)[None, :, None] * 128 + np.arange(128)[:, None, None])
    a = 2 * np.pi * n2 * np.arange(N1)[None, None, :] / NFFT
    twre, twim = np.cos(a), -np.sin(a)
    c["TWA"] = np.tile(twre, (1, 1, 4)).astype(np.float32)
    c["TWB"] = np.tile(twim, (1, 1, 4)).astype(np.float32)
    a = 2 * np.pi * n2 * np.arange(N2)[None, None, :] / N2
    fre, fim = np.cos(a), -np.sin(a)
    c["F2"] = np.stack([fre, fim, -fim, -fre], 1).astype(NPBF)
    gre, gim = np.cos(a), np.sin(a)
    c["G2"] = np.stack([gre, gim, -gim], 1).astype(NPBF)
    a = 2 * np.pi * np.arange(N1)[:, None] * np.arange(N2)[None, :] / NFFT
    c["TWI"] = np.stack([np.cos(a), np.sin(a)], 1).astype(np.float32)
    a = 2 * np.pi * np.arange(N1)[:, None] * np.arange(NR)[None, :] / N1
    c["G1"] = np.stack([np.cos(a) / NFFT, -np.sin(a) / NFFT], 1).astype(NPBF)
    t = np.linspace(0.0, 1.0, L, dtype=np.float32)
    w = (2.0 * np.pi * np.arange(L, dtype=np.float32) / L).astype(np.float32)
    f = np.linspace(1e-4, 15.0, 16, dtype=np.float32)
    emb = np.concatenate([t[:, None], np.cos(f[None, :] * w[:, None]), -np.sin(f[None, :] * w[:, None])], -1).astype(np.float32)
    c["embT"] = np.ascontiguousarray(emb.T)
    c["tv"] = np.ascontiguousarray(np.broadcast_to(t[None, :], (128, L))).astype(np.float32)
    return c


def hy_delta(core, width=2048):
    d = np.abs(np.linspace(MIN_DECAY, MAX_DECAY, width, dtype=np.float32))[core * 256:(core + 1) * 256]
    return np.ascontiguousarray((-d).reshape(2, 128).T).astype(np.float32)


def hyena_phase(nc, kb, psb, hy_s, dr, L, debug=False):
    NR = (L + N2 - 1) // N2
    LP = NR * N2
    NG4 = 32
    F1_d = dr("F1", [NR, 256], BF16); TWA_d = dr("TWA", [128, 3, 512]); TWB_d = dr("TWB", [128, 3, 512])
    F2_d = dr("F2", [128, 4, 3, 384], BF16); G2_d = dr("G2", [128, 3, 3, 384], BF16)
    TWI_d = dr("TWI", [128, 2, 384]); G1_d = dr("G1", [128, 2, NR], BF16)
    emb_d = dr("embT", [33, L]); tv_d = dr("tv", [128, L]); nd_d = dr("nabsd", [128, 2])
    fw1_d = dr("fw1", [33, 64]); fw2_d = dr("fw2", [64, 64]); fw3_d = dr("fw3", [64, 64]); fw4_d = dr("fw4", [64, 1024])
    fb_d = dr("fbf", [64, 4])
    sk_d = dr("skip", [128, 4])
    hcw_d = dr("hcw", [128, 18]); hcb_d = dr("hcb", [128, 6])
    z2_d = dr("z2T", [256, L], BF16, "ExternalOutput")
    zc_s = [dr(f"zc{o}_s", [256, LP], BF16, "Internal") for o in range(2)]
    xc_s = [dr(f"xc{o}_s", [256, LP], F32, "Internal") for o in range(2)]
    hf_s = dr("hf_s", [8, 128, LP], BF16, "ExternalOutput" if debug else "Internal")
    K_s = dr("K_s", [2, 2, NG4, 128, 3072], F32, "Internal")
    if debug:
        zdbg = dr("zdbg", [256, LP], BF16, "ExternalOutput")

    rr = [0]
    held = set()

    def bank(hold=False):
        for _ in range(8):
            b = rr[0] % 8
            rr[0] += 1
            if b not in held:
                if hold:
                    held.add(b)
                return b
        raise AssertionError("all PSUM banks held")

    def release(*bs):
        for b in bs:
            held.discard(b)

    def run_interleaved(gens):
        active = list(gens)
        while active:
            for g in list(active):
                try:
                    next(g)
                except StopIteration:
                    active.remove(g)

    with ExitStack() as st:
        cnt = [0]

        def sb(shape, dt):
            cnt[0] += 1
            return st.enter_context(nc.sbuf_tensor(f"hy{cnt[0]}", list(shape), dt))

        allres = []

        LIVE = ('ones', 'onesb', 'eps', 'zer', 'F1', 'TWA', 'TWB', 'F2', 'G2', 'TWI', 'G1', 'nd', 'fbf', 'sk', 'hcw', 'hcb',
                'negpi', 'zpad', 'zpadb')
        wr_seen = set()

        def refresh():
            allres[:] = [k for k in kb.res.keys() if not k.startswith('ps') and not k.endswith('_s') and k not in LIVE]
            wr_seen.clear()
        refresh()
        st0 = st.enter_context(ExitStack())

        def sb0(shape, dt):
            cnt[0] += 1
            return st0.enter_context(nc.sbuf_tensor(f"hy{cnt[0]}", list(shape), dt))

        def wr(names):
            names = list(names)
            if names[0] in wr_seen:
                return names
            wr_seen.add(names[0])
            return names + allres

        F1 = sb([NR, 256], BF16); TWA = sb([128, 3, 512], F32); TWB = sb([128, 3, 512], F32)
        F2 = sb([128, 4, 3, 384], BF16); G2 = sb([128, 3, 3, 384], BF16); TWI = sb([128, 2, 384], F32); G1 = sb([128, 2, NR], BF16)
        for t_, d_, n_ in ((F1, F1_d, 'F1'), (TWA, TWA_d, 'TWA'), (TWB, TWB_d, 'TWB'), (F2, F2_d, 'F2'), (G2, G2_d, 'G2'), (TWI, TWI_d, 'TWI'), (G1, G1_d, 'G1')):
            kb.dma('sp', t_[:], d_, writes=wr([n_]))
        negpi = sb([128, 1], F32)
        kb.op('pool', lambda e: e.memset(negpi[:], -math.pi), writes=wr(['negpi']))
        zpad = sb([128, N2], F32)
        kb.op('pool', lambda e: e.memset(zpad[:], 0.0), writes=wr(['zpad']))
        zpadb = sb([128, N2], BF16)
        kb.op('pool', lambda e: e.memset(zpadb[:], 0.0), writes=wr(['zpadb']))
        nd = sb([128, 2], F32); fbf = sb([64, 4], F32); sk = sb([128, 4], F32); hcw = sb([128, 18], F32); hcb = sb([128, 6], F32)
        fw1 = sb0([33, 64], F32); fw2 = sb0([64, 64], F32); fw3 = sb0([64, 64], F32); fw4 = sb0([64, 1024], F32)
        for t_, d_, n_ in ((nd, nd_d, 'nd'), (fbf, fb_d, 'fbf'), (sk, sk_d, 'sk'), (hcw, hcw_d, 'hcw'), (hcb, hcb_d, 'hcb'),
                           (fw1, fw1_d, 'fw1'), (fw2, fw2_d, 'fw2'), (fw3, fw3_d, 'fw3'), (fw4, fw4_d, 'fw4')):
            kb.dma('sp', t_[:], d_, writes=wr([n_]))

        kb.mark('H0')
        CH = 2048
        hin = [sb0([128, CH + 2], F32) for _ in range(2)]
        hout = [sb0([128, CH], F32) for _ in range(2)]
        houtb = [sb0([128, CH], BF16) for _ in range(2)]
        it = 0
        for j in range(6):
            for c0 in range(0, L, CH):
                n = min(CH, L - c0)
                i = it % 2
                it += 1
                kb.dma('sp', hin[i][:, 0:n + 2], hy_s[j * 128:(j + 1) * 128, c0:c0 + n + 2], reads=['hy_s'], writes=wr([f'hin{i}']))
                kb.op('dve', lambda e, i=i, n=n, j=j: e.tensor_scalar(out=hout[i][:, 0:n], in0=hin[i][:, 1:n + 1], scalar1=hcw[:, 3 * j + 1:3 * j + 2],
                                                                      scalar2=hcb[:, j:j + 1], op0=ALU.mult, op1=ALU.add),
                      reads=[f'hin{i}', 'hcw', 'hcb'], writes=wr([f'hout{i}']))
                kb.op('dve', lambda e, i=i, n=n, j=j: e.scalar_tensor_tensor(out=hout[i][:, 0:n], in0=hin[i][:, 0:n], scalar=hcw[:, 3 * j:3 * j + 1],
                                                                             in1=hout[i][:, 0:n], op0=ALU.mult, op1=ALU.add),
                      reads=[f'hin{i}', f'hout{i}'], writes=[f'hout{i}'])
                if j < 2:
                    kb.op('dve', lambda e, i=i, n=n, j=j: e.scalar_tensor_tensor(out=houtb[i][:, 0:n], in0=hin[i][:, 2:n + 2], scalar=hcw[:, 3 * j + 2:3 * j + 3],
                                                                                 in1=hout[i][:, 0:n], op0=ALU.mult, op1=ALU.add),
                          reads=[f'hin{i}', f'hout{i}'], writes=wr([f'houtb{i}']))
                    kb.dma('sp', zc_s[0][j * 128:(j + 1) * 128, c0:c0 + n], houtb[i][:, 0:n], reads=[f'houtb{i}'], writes=['zc0_s'])
                else:
                    kb.op('dve', lambda e, i=i, n=n, j=j: e.scalar_tensor_tensor(out=hout[i][:, 0:n], in0=hin[i][:, 2:n + 2], scalar=hcw[:, 3 * j + 2:3 * j + 3],
                                                                                 in1=hout[i][:, 0:n], op0=ALU.mult, op1=ALU.add),
                          reads=[f'hin{i}', f'hout{i}'], writes=[f'hout{i}'])
                    o = (j - 2) // 2
                    ctt = (j - 2) % 2
                    kb.dma('sp', xc_s[o][ctt * 128:(ctt + 1) * 128, c0:c0 + n], hout[i][:, 0:n], reads=[f'hout{i}'], writes=[f'xc{o}_s'])
        if LP > L:
            for j in range(2):
                kb.dma('sp', zc_s[0][j * 128:(j + 1) * 128, L:LP], zpadb[:, 0:LP - L], reads=['zpadb'], writes=['zc0_s'])
                for o in range(2):
                    kb.dma('sp', xc_s[o][j * 128:(j + 1) * 128, L:LP], zpad[:, 0:LP - L], reads=['zpad'], writes=[f'xc{o}_s'])
            for r in range(8):
                kb.dma('sp', hf_s[r, :, L:LP], zpadb[:, 0:LP - L], reads=['zpadb'], writes=['hf_s'])

        kb.mark('H1')
        TC = 512
        embt = [sb0([33, TC], F32) for _ in range(2)]
        tvt = [sb0([128, TC], F32) for _ in range(2)]
        hid = [sb0([64, TC], F32) for _ in range(3)]
        dec = [sb0([128, TC], F32) for _ in range(2)]
        fo = [sb0([128, TC], BF16) for _ in range(3)]
        C1 = math.pi + 16 * math.pi
        fi = 0
        for ci, c0 in enumerate(range(0, L, TC)):
            n = min(TC, L - c0)
            i = ci % 2
            kb.dma('sp', embt[i][:, 0:n], emb_d[:, c0:c0 + n], writes=wr([f'embt{i}']))
            kb.dma('sp', tvt[i][:, 0:n], tv_d[:, c0:c0 + n], writes=wr([f'tvt{i}']))
            src, sres, K = embt[i], f'embt{i}', 33
            for li, wl in enumerate((fw1, fw2, fw3)):
                b = bank()
                kb.op('pe', lambda e, b=b, wl=wl, src=src, K=K, n=n: e.matmul(psb[b][0:64, 0:n], lhsT=wl[0:K, :], rhs=src[0:K, 0:n], start=True, stop=True),
                      reads=[f'fw{li + 1}', sres], writes=[f'ps{b}'])
                h = hid[li % 2]
                hres = f'hid{li % 2}'
                kb.op('dve', lambda e, b=b, h=h, li=li, n=n: e.tensor_scalar(out=h[:, 0:n], in0=psb[b][0:64, 0:n], scalar1=fbf[:, li:li + 1], scalar2=fbf[:, 3:4],
                                                                           op0=ALU.add, op1=ALU.mult),
                      reads=[f'ps{b}', 'fbf'], writes=wr([hres]))
                h2 = hid[2]
                MAGIC = 12582912.0
                kb.op('dve', lambda e, h=h, h2=h2, n=n: e.tensor_scalar(out=h2[:, 0:n], in0=h[:, 0:n], scalar1=1.0 / (2 * math.pi), scalar2=MAGIC, op0=ALU.mult, op1=ALU.add),
                      reads=[hres], writes=wr(['hid2']))
                kb.op('dve', lambda e, h2=h2, n=n: e.tensor_scalar(out=h2[:, 0:n], in0=h2[:, 0:n], scalar1=-MAGIC, scalar2=None, op0=ALU.add),
                      reads=['hid2'], writes=['hid2'])
                kb.op('dve', lambda e, h=h, h2=h2, n=n: e.scalar_tensor_tensor(out=h[:, 0:n], in0=h2[:, 0:n], scalar=-2 * math.pi, in1=h[:, 0:n], op0=ALU.mult, op1=ALU.add),
                      reads=['hid2', hres], writes=[hres])
                kb.op('dve', lambda e, h=h, n=n: e.tensor_scalar(out=h[:, 0:n], in0=h[:, 0:n], scalar1=-3.1415925, scalar2=3.1415925, op0=ALU.max, op1=ALU.min),
                      reads=[hres], writes=[hres])
                kb.op('act', lambda e, h=h, n=n: e.activation(out=h[:, 0:n], in_=h[:, 0:n], func=AF.Sin),
                      reads=[hres], writes=[hres])
                src, sres, K = h, hres, 64
            for ct in range(2):
                kb.op('act', lambda e, ct=ct, i=i, n=n: e.activation(out=dec[ct][:, 0:n], in_=tvt[i][:, 0:n], func=AF.Exp, scale=nd[:, ct:ct + 1]),
                      reads=[f'tvt{i}', 'nd'], writes=wr([f'dec{ct}']))
            for o in range(2):
                for d in range(2):
                    for ct in range(2):
                        r = (o * 2 + d) * 2 + ct
                        b = bank()
                        kb.op('pe', lambda e, b=b, r=r, src=src, n=n: e.matmul(psb[b][:, 0:n], lhsT=fw4[:, r * 128:(r + 1) * 128], rhs=src[:, 0:n], start=True, stop=True),
                              reads=['fw4', sres], writes=[f'ps{b}'])
                        f = fo[fi % 3]; fres = f'fo{fi % 3}'
                        fi += 1
                        if ci == 0:
                            kb.op('dve', lambda e, b=b, ct=ct, n=n: e.tensor_tensor(out=dec[ct][:, 0:n], in0=psb[b][:, 0:n], in1=dec[ct][:, 0:n], op=ALU.mult),
                                  reads=[f'ps{b}', f'dec{ct}'], writes=[f'dec{ct}'])
                            if d == 0:
                                kb.op('dve', lambda e, ct=ct, o=o: e.tensor_tensor(out=dec[ct][:, 0:1], in0=dec[ct][:, 0:1], in1=sk[:, o * 2 + ct:o * 2 + ct + 1], op=ALU.add),
                                      reads=[f'dec{ct}', 'sk'], writes=[f'dec{ct}'])
                            else:
                                kb.op('dve', lambda e, ct=ct: e.memset(dec[ct][:, 0:1], 0.0), reads=[f'dec{ct}'], writes=[f'dec{ct}'])
                            kb.op('dve', lambda e, f=f, ct=ct, n=n: e.tensor_copy(out=f[:, 0:n], in_=dec[ct][:, 0:n]), reads=[f'dec{ct}'], writes=wr([fres]))
                            kb.op('act', lambda e, ct=ct, i=i, n=n: e.activation(out=dec[ct][:, 0:n], in_=tvt[i][:, 0:n], func=AF.Exp, scale=nd[:, ct:ct + 1]),
                                  reads=[f'tvt{i}', 'nd', fres], writes=[f'dec{ct}'])
                        else:
                            kb.op('dve', lambda e, b=b, f=f, ct=ct, n=n: e.tensor_tensor(out=f[:, 0:n], in0=psb[b][:, 0:n], in1=dec[ct][:, 0:n], op=ALU.mult),
                                  reads=[f'ps{b}', f'dec{ct}'], writes=wr([fres]))
                        kb.dma('sp', hf_s[r, :, c0:c0 + n], f[:, 0:n], reads=[fres], writes=['hf_s'], is_output=debug)

        st0.close()
        refresh()
        zt = [sb([NR, 4, N2], BF16) for _ in range(4)]
        m = [sb([128, 512], F32) for _ in range(8)]
        Bb = [sb([128, 3, 2, 4, 128], BF16) for _ in range(4)]
        NKT_ = 3
        Kt = [sb([128, 3072], F32) for _ in range(NKT_)]
        Pb = [sb([128, 3, 2, 4, 128], BF16) for _ in range(2)]
        Eb = [sb([128, 2, 4, N2], BF16) for _ in range(2)]
        xt = [sb([NR, 4, N2], F32) for _ in range(3)]
        zo = [sb([NR, 4, N2], BF16) for _ in range(2)]
        ctr = dict(zt=0, m=0, B=0, K=0, P=0, E=0, x=0, zo=0)

        def nxt(name, n):
            v = ctr[name] % n
            ctr[name] += 1
            return v

        def mtile():
            i = nxt('m', 8)
            return m[i], f'm{i}'

        def load_zt(src_rows, sres):
            i = nxt('zt', 4)
            kb.dma('sp', zt[i][:], src_rows.rearrange("c (a b) -> a c b", b=N2), reads=[sres], writes=wr([f'zt{i}']))
            return zt[i], f'zt{i}'

        def fwd_S1_TW(z, zres):
            bi = nxt('B', 4)
            B = Bb[bi]; bres = f'B{bi}'
            for j in range(3):
                for cp in range(2):
                    b = bank()
                    for cc in range(2):
                        c = cp * 2 + cc
                        kb.op('pe', lambda e, b=b, cc=cc, c=c, j=j: e.matmul(psb[b][:, cc * 256:(cc + 1) * 256], lhsT=z[:, c, j * 128:(j + 1) * 128], rhs=F1[:, :],
                                                                            start=True, stop=True),
                              reads=[zres, 'F1'], writes=[f'ps{b}'])
                    m1, r1 = mtile(); m2, r2 = mtile()
                    kb.op('dve', lambda e, b=b, m1=m1, j=j: e.tensor_tensor(out=m1[:], in0=psb[b][:, :], in1=TWA[:, j, :], op=ALU.mult),
                          reads=[f'ps{b}', 'TWA'], writes=wr([r1]))
                    kb.op('dve', lambda e, b=b, m2=m2, j=j: e.tensor_tensor(out=m2[:], in0=psb[b][:, :], in1=TWB[:, j, :], op=ALU.mult),
                          reads=[f'ps{b}', 'TWB'], writes=wr([r2]))
                    v1 = m1[:].rearrange("p (c r k) -> p c r k", c=2, r=2)
                    v2 = m2[:].rearrange("p (c r k) -> p c r k", c=2, r=2)
                    kb.op('pool', lambda e, B=B, j=j, cp=cp, v1=v1, v2=v2: e.tensor_tensor(out=B[:, j, 0, cp * 2:cp * 2 + 2, :], in0=v1[:, :, 0, :], in1=v2[:, :, 1, :], op=ALU.subtract),
                          reads=[r1, r2], writes=wr([bres]))
                    kb.op('pool', lambda e, B=B, j=j, cp=cp, v1=v1, v2=v2: e.tensor_tensor(out=B[:, j, 1, cp * 2:cp * 2 + 2, :], in0=v2[:, :, 0, :], in1=v1[:, :, 1, :], op=ALU.add),
                          reads=[r1, r2], writes=[bres])
                    yield
            return B, bres

        def S2(terms, jj):
            bre, bim = bank(True), bank(True)
            nt = len(terms) * 3 * 2
            i = 0
            for (B, bres, sgn) in terms:
                for j in range(3):
                    pr = ((0, 0), (2, 1))
                    pi2 = ((1, 0), (0, 1)) if sgn > 0 else ((2, 0), (3, 1))
                    for t in range(2):
                        for out_b, (fidx, ri) in ((bre, pr[t]), (bim, pi2[t])):
                            kb.op('pe', lambda e, out_b=out_b, fidx=fidx, ri=ri, j=j, B=B, i=i: e.matmul(
                                psb[out_b][:, :], lhsT=F2[:, fidx, j, jj * 128:(jj + 1) * 128], rhs=B[:, j, ri, :, :],
                                start=(i == 0), stop=(i == nt - 1)),
                                reads=['F2', bres], writes=[f'ps{out_b}'])
                        i += 1
                    yield
            return bre, bim

        kb.mark('H2')
        def h2_A(ctx):
            o, ct, g = ctx['key']
            zf, zfr = load_zt(hf_s[(o * 2 + 0) * 2 + ct, g * 4:(g + 1) * 4, :], 'hf_s')
            zb, zbr = load_zt(hf_s[(o * 2 + 1) * 2 + ct, g * 4:(g + 1) * 4, :], 'hf_s')
            ctx['Bf'] = yield from fwd_S1_TW(zf, zfr)
            ctx['Bk'] = yield from fwd_S1_TW(zb, zbr)

        def h2_B(ctx):
            o, ct, g = ctx['key']
            (Bf, bfr), (Bk, bkr) = ctx['Bf'], ctx['Bk']
            ki = nxt('K', NKT_)
            K = Kt[ki]; kres = f'K{ki}'
            for jj in range(3):
                bre, bim = yield from S2([(Bf, bfr, 1), (Bk, bkr, -1)], jj)
                kb.op('act', lambda e, K=K, jj=jj, bre=bre: e.copy(out=K[:, (jj * 2) * 512:(jj * 2 + 1) * 512], in_=psb[bre][:, :]),
                      reads=[f'ps{bre}'], writes=wr([kres]))
                kb.op('act', lambda e, K=K, jj=jj, bim=bim: e.copy(out=K[:, (jj * 2 + 1) * 512:(jj * 2 + 2) * 512], in_=psb[bim][:, :]),
                      reads=[f'ps{bim}'], writes=[kres])
                release(bre, bim)
                yield
            kb.dma('act', K_s[o, ct, g], K[:], reads=[kres], writes=['K_s'])

        keys = [(o, ct, g) for o in range(2) for ct in range(2) for g in range(NG4)]
        ctxs = [dict(key=k) for k in keys]
        for i in range(len(keys) + 1):
            gens = []
            if i < len(keys):
                gens.append(h2_A(ctxs[i]))
            if i - 1 >= 0:
                gens.append(h2_B(ctxs[i - 1]))
            run_interleaved(gens)

        kb.mark('H3')
        def h3_A(ctx):
            o, ct, g = ctx['key']
            rows = ctx['rows']
            z, zres = load_zt(zc_s[o][rows, :], f'zc{o}_s')
            ki = nxt('K', NKT_)
            K = Kt[ki]; kres = f'K{ki}'
            kb.dma('sp', K[:], K_s[o, ct, g], reads=['K_s'], writes=[kres])
            ctx['K'] = (K, kres)
            ctx['B'] = yield from fwd_S1_TW(z, zres)

        def h3_B(ctx):
            o, ct, g = ctx['key']
            rows = ctx['rows']
            K, kres = ctx['K']
            B, bres = ctx['B']
            xi = nxt('x', 3)
            X = xt[xi]; xres = f'x{xi}'
            kb.dma('sp', X[:], xc_s[o][rows, :].rearrange("c (a b) -> a c b", b=N2), reads=[f'xc{o}_s'], writes=wr([xres]))
            ctx['X'] = (X, xres)
            pi_ = nxt('P', 2)
            P = Pb[pi_]; pres = f'P{pi_}'
            ctx['P'] = (P, pres)
            for jj in range(3):
                bre, bim = yield from S2([(B, bres, 1)], jj)
                Kre = K[:, (jj * 2) * 512:(jj * 2 + 1) * 512]
                Kim = K[:, (jj * 2 + 1) * 512:(jj * 2 + 2) * 512]
                ms = [mtile() for _ in range(4)]
                for (mt, mr), (pb, kk) in zip(ms, ((bre, Kre), (bim, Kim), (bre, Kim), (bim, Kre))):
                    kb.op('dve', lambda e, mt=mt, pb=pb, kk=kk: e.tensor_tensor(out=mt[:], in0=psb[pb][:, :], in1=kk, op=ALU.mult),
                          reads=[f'ps{pb}', kres], writes=wr([mr]))
                kb.op('pool', lambda e, P=P, jj=jj, ms=ms: e.tensor_tensor(out=P[:, jj, 0, :, :], in0=ms[0][0][:].rearrange("p (c k) -> p c k", c=4),
                                                                         in1=ms[1][0][:].rearrange("p (c k) -> p c k", c=4), op=ALU.subtract),
                      reads=[ms[0][1], ms[1][1]], writes=wr([pres]))
                kb.op('pool', lambda e, P=P, jj=jj, ms=ms: e.tensor_tensor(out=P[:, jj, 1, :, :], in0=ms[2][0][:].rearrange("p (c k) -> p c k", c=4),
                                                                         in1=ms[3][0][:].rearrange("p (c k) -> p c k", c=4), op=ALU.add),
                      reads=[ms[2][1], ms[3][1]], writes=[pres])
                release(bre, bim)
                yield

        def h3_C(ctx):
            P, pres = ctx['P']
            ei = nxt('E', 2)
            E = Eb[ei]; eres = f'E{ei}'
            ctx['E'] = (E, eres)
            for c in range(4):
                bre, bim = bank(True), bank(True)
                i = 0
                for jj in range(3):
                    for t in range(2):
                        for out_b, (ri, gidx) in ((bre, ((0, 0), (1, 2))[t]), (bim, ((0, 1), (1, 0))[t])):
                            kb.op('pe', lambda e, out_b=out_b, jj=jj, ri=ri, gidx=gidx, c=c, i=i, P=P: e.matmul(
                                psb[out_b][:, 0:N2], lhsT=P[:, jj, ri, c, :], rhs=G2[:, gidx, jj, :], start=(i == 0), stop=(i == 5)),
                                reads=[pres, 'G2'], writes=[f'ps{out_b}'])
                        i += 1
                yield
                ms = [mtile() for _ in range(4)]
                for (mt, mr), (pb, tw) in zip(ms, ((bre, 0), (bim, 1), (bre, 1), (bim, 0))):
                    kb.op('dve', lambda e, mt=mt, pb=pb, tw=tw: e.tensor_tensor(out=mt[:, 0:N2], in0=psb[pb][:, 0:N2], in1=TWI[:, tw, :], op=ALU.mult),
                          reads=[f'ps{pb}', 'TWI'], writes=wr([mr]))
                kb.op('pool', lambda e, E=E, c=c, ms=ms: e.tensor_tensor(out=E[:, 0, c, :], in0=ms[0][0][:, 0:N2], in1=ms[1][0][:, 0:N2], op=ALU.subtract),
                      reads=[ms[0][1], ms[1][1]], writes=wr([eres]))
                kb.op('pool', lambda e, E=E, c=c, ms=ms: e.tensor_tensor(out=E[:, 1, c, :], in0=ms[2][0][:, 0:N2], in1=ms[3][0][:, 0:N2], op=ALU.add),
                      reads=[ms[2][1], ms[3][1]], writes=[eres])
                release(bre, bim)
                yield

        def h3_D(ctx):
            o, ct, g = ctx['key']
            rows = ctx['rows']
            E, eres = ctx['E']
            X, xres = ctx['X']
            zi = nxt('zo', 2)
            ZO = zo[zi]; zores = f'zo{zi}'
            for c in range(4):
                b = bank()
                for ri in range(2):
                    kb.op('pe', lambda e, b=b, ri=ri, c=c, E=E: e.matmul(psb[b][0:NR, 0:N2], lhsT=G1[:, ri, :], rhs=E[:, ri, c, :], start=(ri == 0), stop=(ri == 1)),
                          reads=['G1', eres], writes=[f'ps{b}'])
                kb.op('dve', lambda e, b=b, c=c, ZO=ZO, X=X: e.tensor_tensor(out=ZO[:, c, :], in0=psb[b][0:NR, 0:N2], in1=X[:, c, :], op=ALU.mult),
                      reads=[f'ps{b}', xres], writes=wr([zores]))
                yield
            if o == 0:
                kb.dma('act', zc_s[1][rows, :].rearrange("c (a b) -> a c b", b=N2), ZO[:], reads=[zores], writes=['zc1_s'], is_output=False)
                if debug:
                    kb.dma('act', zdbg[rows, :].rearrange("c (a b) -> a c b", b=N2), ZO[:], reads=[zores], is_output=True)
            else:
                nfr = L // N2
                if nfr:
                    kb.dma('act', z2_d[rows, 0:nfr * N2].rearrange("c (a b) -> a c b", b=N2), ZO[0:nfr, :, :], reads=[zores], is_output=True)
                if L % N2:
                    kb.dma('act', z2_d[rows, nfr * N2:L].rearrange("(a c) b -> a c b", a=1), ZO[nfr:nfr + 1, :, 0:L % N2], reads=[zores], is_output=True)

        for o in range(2):
            keys = [(o, ct, g) for ct in range(2) for g in range(NG4)]
            ctxs = [dict(key=k, rows=slice(k[1] * 128 + k[2] * 4, k[1] * 128 + k[2] * 4 + 4)) for k in keys]
            n = len(keys)
            for i in range(n + 3):
                gens = []
                if i < n:
                    gens.append(h3_A(ctxs[i]))
                if 0 <= i - 1 < n:
                    gens.append(h3_B(ctxs[i - 1]))
                if 0 <= i - 2 < n:
                    gens.append(h3_C(ctxs[i - 2]))
                if 0 <= i - 3 < n:
                    gens.append(h3_D(ctxs[i - 3]))
                run_interleaved(gens)


def layout_hT(hT, TG=256):
    D, L = hT.shape
    KD = D // 128
    NGRP = (L + TG - 1) // TG
    p = np.zeros((D, NGRP * TG), hT.dtype)
    p[:, :L] = hT
    p = p.reshape(KD, 128, NGRP, TG).transpose(2, 1, 0, 3)
    return np.ascontiguousarray(p).reshape(NGRP, 128, KD * TG)


def build_A(D, L, TG=256, eps=1e-6, hyena=None, debug=False):
    KD = D // 128
    HD = 128
    NQ = 2
    NHY = 6
    NCOL = NQ * HD + 2 * HD + NHY * 128
    nc = bass.Bass("TRN2", target_bir_lowering=False)
    dr = lambda n, s, dt=F32, kind="ExternalInput": nc.dram_tensor(n, list(s), dt, kind=kind).ap()
    NGRP = (L + TG - 1) // TG
    hT_d = dr("hT", [NGRP, 128, KD * TG])
    w_d = dr("w", [D, NCOL])
    gm_d = dr("gmix", [128, KD])
    qk_d = dr("qkn", [128, 2])
    cos_d = dr("cosT", [128, L])
    sin_d = dr("sinT", [128, L])
    rT_d = dr("rT", [128, 128])
    ya_d = dr("yaT", [NQ * HD, L], BF16, "ExternalOutput")
    qT_s = dr("qT_s", [NQ, HD, L], BF16, "Internal")
    kT_s = dr("kT_s", [HD, L], BF16, "Internal")
    v_s = dr("v_s", [L, HD], BF16, "Internal")
    hy_s = dr("hy_s", [NHY * 128, L + 2], F32, "ExternalOutput" if debug else "Internal")
    if debug:
        qdbg = dr("qdbg", [NQ, HD, L], BF16, "ExternalOutput")
        kdbg = dr("kdbg", [HD, L], BF16, "ExternalOutput")
        vdbg = dr("vdbg", [L, HD], BF16, "ExternalOutput")

    groups = [(t0, min(TG, L - t0)) for t0 in range(0, L, TG)]
    with ExitStack() as stack:
        kb = KB(nc, stack)
        psb = [kb.ps([128, 512]) for _ in range(8)]
        ones = kb.sb([128, 128], F32)
        onesb = kb.sb([128, 128], BF16)
        epst = kb.sb([128, 1], F32)
        zer = kb.sb([128, 8], F32)
        kb.op('pool', lambda e: e.memset(ones[:], 1.0), writes=['ones'])
        kb.op('pool', lambda e: e.memset(onesb[:], 1.0), writes=['onesb'])
        kb.op('pool', lambda e: e.memset(epst[:], eps), writes=['eps'])
        kb.op('pool', lambda e: e.memset(zer[:], 0.0), writes=['zer'])
        for r in range(NHY):
            kb.dma('sp', hy_s[r * 128:(r + 1) * 128, 0:1], zer[:, 0:1], reads=['zer'], writes=['hy_s'], allow_slow_non_contiguous=True)
            kb.dma('sp', hy_s[r * 128:(r + 1) * 128, L + 1:L + 2], zer[:, 0:1], reads=['zer'], writes=['hy_s'], allow_slow_non_contiguous=True)

        kb.mark('A1')
        with ExitStack() as st1:
            sb1 = lambda shape, dt: st1.enter_context(nc.sbuf_tensor(f"a1_{kb.nbuf}_{(kb.__setattr__('nbuf', kb.nbuf + 1))}", list(shape), dt))
            W = sb1([128, KD, NCOL], BF16)
            gm = sb1([128, KD], F32)
            qkn = sb1([128, 2], F32)
            rT = sb1([128, 128], F32)
            hT = [sb1([128, KD, TG], F32) for _ in range(2)]
            uTs = [sb1([128, KD, TG], BF16) for _ in range(2)]
            sqs = [sb1([128, 4, TG], F32) for _ in range(2)]
            acc = sb1([128, TG], F32); accp = sb1([128, TG], F32); rstd = sb1([128, TG], F32)
            cosbs = [sb1([128, TG], F32) for _ in range(2)]; sinbs = [sb1([128, TG], F32) for _ in range(2)]
            xs = [sb1([128, TG], F32) for _ in range(3)]
            x2 = [sb1([128, TG], F32) for _ in range(3)]
            rs2 = [sb1([128, TG], F32) for _ in range(3)]
            qo = [sb1([128, TG], BF16) for _ in range(3)]
            hyb = [sb1([128, TG], F32) for _ in range(3)]
            vb = [sb1([128, HD], BF16) for _ in range(2)]
            kb.dma('sp', gm[:], gm_d[:, :], writes=['gm'])
            kb.dma('sp', qkn[:], qk_d[:, :], writes=['qkn'])
            kb.dma('sp', rT[:], rT_d[:, :], writes=['rT'])
            for k0 in range(0, KD, 8):
                nk = min(8, KD - k0)
                for c0 in range(0, NCOL, 512):
                    cc = min(512, NCOL - c0)
                    kb.dma('pool', W[:, k0:k0 + nk, c0:c0 + cc],
                           w_d[k0 * 128:(k0 + nk) * 128, c0:c0 + cc].rearrange("(k p) n -> p k n", p=128), writes=['W'])
            def stage_N(gi, t0, tg):
                hb = hT[gi % 2]; hres = f'hT{gi % 2}'
                uT = uTs[gi % 2]; ures = f'uT{gi % 2}'
                cosb = cosbs[gi % 2]; sinb = sinbs[gi % 2]; cres = f'cosb{gi % 2}'; sres_ = f'sinb{gi % 2}'
                kb.dma('sp', hb[:, :, 0:tg], hT_d[gi].rearrange("p (k t) -> p k t", t=TG)[:, :, 0:tg], writes=[hres])
                kb.dma('sp', cosb[:, 0:tg], cos_d[:, t0:t0 + tg], writes=[cres])
                kb.dma('sp', sinb[:, 0:tg], sin_d[:, t0:t0 + tg], writes=[sres_])
                for pi_, p0 in enumerate(range(0, KD, 4)):
                    npc = min(4, KD - p0)
                    sq = sqs[pi_ % 2]; sqres = f'sq{pi_ % 2}'
                    kb.op('act', lambda e, p0=p0, npc=npc, hb=hb, tg=tg, sq=sq: e.activation(out=sq[:, 0:npc, 0:tg], in_=hb[:, p0:p0 + npc, 0:tg], func=AF.Square),
                          reads=[hres], writes=[sqres])
                    for c in range(npc):
                        if p0 == 0 and c == 0:
                            kb.op('pool', lambda e, sq=sq, tg=tg: e.tensor_copy(out=acc[:, 0:tg], in_=sq[:, 0, 0:tg]), reads=[sqres], writes=['acc'])
                        else:
                            kb.op('pool', lambda e, sq=sq, c=c, tg=tg: e.tensor_tensor(out=acc[:, 0:tg], in0=acc[:, 0:tg], in1=sq[:, c, 0:tg], op=ALU.add),
                                  reads=[sqres, 'acc'], writes=['acc'])
                yield
                kb.op('pe', lambda e, tg=tg: e.matmul(psb[7][:, 0:tg], lhsT=ones[:], rhs=acc[:, 0:tg], start=True, stop=True),
                      reads=['ones', 'acc'], writes=['ps7'])
                kb.op('act', lambda e, tg=tg: e.activation(out=rstd[:, 0:tg], in_=psb[7][:, 0:tg], func=AF.Sqrt, bias=epst[:], scale=1.0 / D),
                      reads=['ps7', 'eps'], writes=['rstd'])
                kb.op('dve', lambda e, tg=tg: e.reciprocal(out=rstd[:, 0:tg], in_=rstd[:, 0:tg]), reads=['rstd'], writes=['rstd'])
                for k in range(KD):
                    kb.op('dve', lambda e, k=k, hb=hb, tg=tg, uT=uT: e.scalar_tensor_tensor(out=uT[:, k, 0:tg], in0=hb[:, k, 0:tg], scalar=gm[:, k:k + 1],
                                                                                   in1=rstd[:, 0:tg], op0=ALU.mult, op1=ALU.mult),
                          reads=[hres, 'gm', 'rstd'], writes=[ures])

            def stage_M(gi, t0, tg):
                hb = hT[gi % 2]; hres = f'hT{gi % 2}'
                uT = uTs[gi % 2]; ures = f'uT{gi % 2}'
                cosb = cosbs[gi % 2]; sinb = sinbs[gi % 2]; cres = f'cosb{gi % 2}'; sres_ = f'sinb{gi % 2}'
                for k in range(KD):
                    for j in range(3):
                        kb.op('pe', lambda e, j=j, k=k, tg=tg, uT=uT: e.matmul(psb[j][:, 0:tg], lhsT=W[:, k, j * 128:(j + 1) * 128], rhs=uT[:, k, 0:tg],
                                                                       start=(k == 0), stop=(k == KD - 1)),
                              reads=['W', ures], writes=[f'ps{j}'])
                for j in range(3):
                    kb.op('act', lambda e, j=j, tg=tg: e.copy(out=xs[j][:, 0:tg], in_=psb[j][:, 0:tg]), reads=[f'ps{j}'], writes=[f'xs{j}'])
                    kb.op('act', lambda e, j=j, tg=tg: e.activation(out=x2[j][:, 0:tg], in_=psb[j][:, 0:tg], func=AF.Square), reads=[f'ps{j}'], writes=[f'x2{j}'])

                def hy_pass(js):
                    for k in range(KD):
                        for j in js:
                            b = 3 + (j % 3)
                            kb.op('pe', lambda e, j=j, k=k, b=b, tg=tg, uT=uT: e.matmul(psb[b][:, 0:tg], lhsT=W[:, k, (4 + j) * 128:(5 + j) * 128], rhs=uT[:, k, 0:tg],
                                                                                start=(k == 0), stop=(k == KD - 1)),
                                  reads=['W', ures], writes=[f'ps{b}'])
                    for j in js:
                        b = 3 + (j % 3)
                        kb.op('act' if j % 2 == 0 else 'dve',
                              (lambda e, j=j, b=b, tg=tg: e.copy(out=hyb[j % 3][:, 0:tg], in_=psb[b][:, 0:tg])) if j % 2 == 0 else
                              (lambda e, j=j, b=b, tg=tg: e.tensor_copy(out=hyb[j % 3][:, 0:tg], in_=psb[b][:, 0:tg])),
                              reads=[f'ps{b}'], writes=[f'hyb{j % 3}'])
                        kb.dma('sp', hy_s[j * 128:(j + 1) * 128, 1 + t0:1 + t0 + tg], hyb[j % 3][:, 0:tg], reads=[f'hyb{j % 3}'], writes=['hy_s'],
                               is_output=debug)

                hy_pass((0, 1, 2))
                for s0 in range(0, tg, 128):
                    sn = min(128, tg - s0)
                    vi = (s0 // 128) % 2
                    for k in range(KD):
                        kb.op('pe', lambda e, k=k, s0=s0, sn=sn, uT=uT: e.matmul(psb[6][0:sn, 0:HD], lhsT=uT[:, k, s0:s0 + sn], rhs=W[:, k, 3 * 128:4 * 128],
                                                                         start=(k == 0), stop=(k == KD - 1)),
                              reads=['W', ures], writes=['ps6'])
                    kb.op('act', lambda e, vi=vi, sn=sn: e.copy(out=vb[vi][0:sn, :], in_=psb[6][0:sn, 0:HD]), reads=['ps6'], writes=[f'vb{vi}'])
                    kb.dma('sp', v_s[t0 + s0:t0 + s0 + sn, :], vb[vi][0:sn, :], reads=[f'vb{vi}'], writes=['v_s'])
                    if debug:
                        kb.dma('sp', vdbg[t0 + s0:t0 + s0 + sn, :], vb[vi][0:sn, :], reads=[f'vb{vi}'], is_output=True)
                yield
                hy_pass((3, 4, 5))
                for j in range(3):
                    kb.op('pe', lambda e, j=j, tg=tg: e.matmul(psb[j][:, 0:tg], lhsT=ones[:], rhs=x2[j][:, 0:tg], start=True, stop=True),
                          reads=['ones', f'x2{j}'], writes=[f'ps{j}'])
                for j in range(3):
                    gcol = 0 if j < 2 else 1
                    kb.op('act', lambda e, j=j, tg=tg: e.activation(out=rs2[j][:, 0:tg], in_=psb[j][:, 0:tg], func=AF.Sqrt, bias=epst[:], scale=1.0 / HD),
                          reads=[f'ps{j}', 'eps'], writes=[f'rs2{j}'])
                    kb.op('dve', lambda e, j=j, tg=tg: e.reciprocal(out=rs2[j][:, 0:tg], in_=rs2[j][:, 0:tg]), reads=[f'rs2{j}'], writes=[f'rs2{j}'])
                    kb.op('dve', lambda e, j=j, tg=tg, gcol=gcol: e.scalar_tensor_tensor(out=xs[j][:, 0:tg], in0=xs[j][:, 0:tg], scalar=qkn[:, gcol:gcol + 1],
                                                                                       in1=rs2[j][:, 0:tg], op0=ALU.mult, op1=ALU.mult),
                          reads=[f'xs{j}', f'rs2{j}', 'qkn'], writes=[f'xs{j}'])
                for j in range(3):
                    kb.op('pe', lambda e, j=j, tg=tg: e.matmul(psb[j][:, 0:tg], lhsT=rT[:], rhs=xs[j][:, 0:tg], start=True, stop=True),
                          reads=['rT', f'xs{j}'], writes=[f'ps{j}'])
                for j in range(3):
                    kb.op('dve', lambda e, j=j, tg=tg, sinb=sinb: e.tensor_tensor(out=x2[j][:, 0:tg], in0=psb[j][:, 0:tg], in1=sinb[:, 0:tg], op=ALU.mult),
                          reads=[f'ps{j}', sres_], writes=[f'x2{j}'])
                    kb.op('dve', lambda e, j=j, tg=tg, cosb=cosb: e.tensor_tensor(out=xs[j][:, 0:tg], in0=xs[j][:, 0:tg], in1=cosb[:, 0:tg], op=ALU.mult),
                          reads=[f'xs{j}', cres], writes=[f'xs{j}'])
                    kb.op('dve', lambda e, j=j, tg=tg: e.tensor_tensor(out=qo[j][:, 0:tg], in0=xs[j][:, 0:tg], in1=x2[j][:, 0:tg], op=ALU.add),
                          reads=[f'xs{j}', f'x2{j}'], writes=[f'qo{j}'])
                    dst = qT_s[j, :, t0:t0 + tg] if j < 2 else kT_s[:, t0:t0 + tg]
                    kb.dma('sp', dst, qo[j][:, 0:tg], reads=[f'qo{j}'], writes=['qk_s'])
                    if debug:
                        dst = qdbg[j, :, t0:t0 + tg] if j < 2 else kdbg[:, t0:t0 + tg]
                        kb.dma('sp', dst, qo[j][:, 0:tg], reads=[f'qo{j}'], is_output=True)

            n0 = stage_N(0, *groups[0])
            for _ in n0:
                pass
            for gi, (t0, tg) in enumerate(groups):
                nn = stage_N(gi + 1, *groups[gi + 1]) if gi + 1 < len(groups) else iter(())
                mm_ = stage_M(gi, t0, tg)
                next(nn, None)
                next(mm_, None)
                for _ in nn:
                    pass
                for _ in mm_:
                    pass

        kb.mark('ATT')
        NKT = (L + 127) // 128
        QB = 512
        scale = HD ** -0.5
        with ExitStack() as st2:
            sb2 = lambda shape, dt: st2.enter_context(nc.sbuf_tensor(f"a2_{kb.nbuf}_{(kb.__setattr__('nbuf', kb.nbuf + 1))}", list(shape), dt))
            KT = sb2([128, L], BF16)
            QT = sb2([128, NQ, L], BF16)
            V = sb2([128, NKT, HD], BF16)
            NS = 4
            LOOK = 3
            pT = [sb2([128, QB], BF16) for _ in range(NS)]
            rc = sb2([128, QB], F32)
            accD = sb2([128, QB], F32); accP = sb2([128, QB], F32)
            ob = [sb2([128, QB], BF16) for _ in range(2)]
            a1res = ['W', 'uT0', 'uT1', 'hT0', 'hT1', 'sq0', 'sq1', 'acc', 'accp', 'rstd', 'cosb0', 'cosb1', 'sinb0', 'sinb1', 'gm', 'qkn', 'rT'] + \
                    [f'{n}{j}' for n in ('xs', 'x2', 'rs2', 'qo') for j in range(3)] + [f'hyb{j}' for j in range(3)] + ['vb0', 'vb1']
            kb.dma('sp', KT[:], kT_s[:, :], reads=['qk_s'], writes=['KT'] + a1res)
            for h in range(NQ):
                kb.dma('sp', QT[:, h, :], qT_s[h], reads=['qk_s'], writes=['QT'] + a1res)
            nfull = L // 128
            for kt0 in range(0, nfull, 16):
                kn_ = min(16, nfull - kt0)
                kb.dma('sp', V[:, kt0:kt0 + kn_, :], v_s[kt0 * 128:(kt0 + kn_) * 128, :].rearrange("(kt p) d -> p kt d", p=128), reads=['v_s'], writes=['V'] + a1res)
            if L % 128:
                kb.dma('sp', V[0:L % 128, nfull, :], v_s[nfull * 128:L, :], reads=['v_s'], writes=['V'] + a1res)
            it = 0
            blk = 0
            for h in range(NQ):
                for bi, q0 in enumerate(range(0, L, QB)):
                    qn = min(QB, L - q0)
                    ob_i = blk % 2
                    po, pd = (4, 5) if (blk % 2 == 0) else (6, 7)
                    blk += 1
                    kb.op('dve', lambda e, qn=qn: e.memset(accD[:, 0:qn], 0.0), writes=['accD'] + a1res)
                    kb.op('pool', lambda e, qn=qn: e.memset(accP[:, 0:qn], 0.0), writes=['accP'] + a1res)

                    def s_stage(kt, h=h, q0=q0, qn=qn):
                        kn = min(128, L - kt * 128)
                        sbk = kt % NS
                        kb.op('pe', lambda e: e.matmul(
                            psb[sbk][0:kn, 0:qn], lhsT=KT[:, kt * 128:kt * 128 + kn], rhs=QT[:, h, q0:q0 + qn], start=True, stop=True),
                            reads=['KT', 'QT'], writes=[f'ps{sbk}'])
                        kb.op('act', lambda e: e.activation(out=pT[sbk][0:kn, 0:qn], in_=psb[sbk][0:kn, 0:qn], func=AF.Exp, scale=scale),
                              reads=[f'ps{sbk}'], writes=[f'pT{sbk}'])

                    def pv_stage(kt, qn=qn, po=po):
                        kn = min(128, L - kt * 128)
                        pi = kt % NS
                        kb.op('pe', lambda e: e.matmul(
                            psb[po][:, 0:qn], lhsT=V[0:kn, kt, :], rhs=pT[pi][0:kn, 0:qn], start=(kt == 0), stop=(kt == NKT - 1)),
                            reads=['V', f'pT{pi}'], writes=[f'ps{po}'])
                        if kt % 2 == 0:
                            kb.op('dve', lambda e: e.tensor_tensor(out=accD[0:kn, 0:qn], in0=accD[0:kn, 0:qn], in1=pT[pi][0:kn, 0:qn], op=ALU.add),
                                  reads=['accD', f'pT{pi}'], writes=['accD'])
                        else:
                            kb.op('pool', lambda e: e.tensor_tensor(out=accP[0:kn, 0:qn], in0=accP[0:kn, 0:qn], in1=pT[pi][0:kn, 0:qn], op=ALU.add),
                                  reads=['accP', f'pT{pi}'], writes=['accP'])

                    for i in range(NKT + LOOK):
                        if i < NKT:
                            s_stage(i)
                        if i - LOOK >= 0:
                            pv_stage(i - LOOK)
                    kb.op('pe', lambda e, qn=qn, pd=pd: e.matmul(psb[pd][:, 0:qn], lhsT=ones[:], rhs=accD[:, 0:qn], start=True, stop=False),
                          reads=['ones', 'accD'], writes=[f'ps{pd}'])
                    kb.op('pe', lambda e, qn=qn, pd=pd: e.matmul(psb[pd][:, 0:qn], lhsT=ones[:], rhs=accP[:, 0:qn], start=False, stop=True),
                          reads=['ones', 'accP'], writes=[f'ps{pd}'])
                    kb.op('dve', lambda e, qn=qn, pd=pd: e.reciprocal(out=rc[:, 0:qn], in_=psb[pd][:, 0:qn]), reads=[f'ps{pd}'], writes=['rc'])
                    kb.op('dve', lambda e, qn=qn, po=po, ob_i=ob_i: e.tensor_tensor(out=ob[ob_i][:, 0:qn], in0=psb[po][:, 0:qn], in1=rc[:, 0:qn], op=ALU.mult),
                          reads=[f'ps{po}', 'rc'], writes=[f'ob{ob_i}'])
                    kb.dma('act', ya_d[h * HD:(h + 1) * HD, q0:q0 + qn], ob[ob_i][:, 0:qn], reads=[f'ob{ob_i}'], is_output=True)
        if hyena:
            hyena_phase(nc, kb, psb, hy_s, dr, L, debug)
        kb.mark('END')
        nc._kb_marks = kb.marks
        kb.emit()
    return nc


def build_B(D, AW, DFF, NG, TI, eps=1e-6):
    T = TI + 2
    KD, KA, KF = D // 128, AW // 128, DFF // 128
    nc = bass.Bass("TRN2", target_bir_lowering=False)
    dr = lambda n, s, dt=F32, kind="ExternalInput": nc.dram_tensor(n, list(s), dt, kind=kind).ap()
    hT_d = dr("hT", [NG, D, T])
    ya_d = dr("yaT", [NG, AW, T], BF16)
    zb_d = dr("zbT", [NG, AW, T], BF16)
    wg_d = dr("wg", [D, 2 * D])
    wa_d = dr("wa", [AW, D])
    wb_d = dr("wb", [AW, D])
    wo_d = dr("wo", [D, D])
    wfg_d = dr("wfg", [D, DFF])
    wfu_d = dr("wfu", [D, DFF])
    wfd_d = dr("wfd", [DFF, D])
    gm_d = dr("gmix", [128, KD])
    gf_d = dr("gffn", [128, KD])
    gl_d = dr("gfin", [128, KD])
    cw_d = dr("cw", [128, KF * 3])
    cb_d = dr("cb", [128, KF])
    out_d = dr("outT", [NG, D, TI], F32, "ExternalOutput")

    with ExitStack() as stack:
        kb = KB(nc, stack)
        hT = kb.sb([128, KD, T], F32)
        uT = kb.sb([128, KD, T], BF16)
        NBIG = max((KD + 2 * KA) * T, KF * T)
        big = kb.sb([128, NBIG], BF16)
        mixT = big[:, 0:KD * T].rearrange("p (k t) -> p k t", t=T)
        yaT = big[:, KD * T:(KD + KA) * T].rearrange("p (k t) -> p k t", t=T)
        zbT = big[:, (KD + KA) * T:(KD + 2 * KA) * T].rearrange("p (k t) -> p k t", t=T)
        actT = big[:, 0:KF * T].rearrange("p (k t) -> p k t", t=T)
        NKW = 8
        NW = 4
        wbuf = [kb.sb([128, NKW, 512], BF16) for _ in range(NW)]
        gm = kb.sb([128, KD], F32); gf = kb.sb([128, KD], F32); gl = kb.sb([128, KD], F32)
        cw = kb.sb([128, KF * 3], F32); cb = kb.sb([128, KF], F32)
        ones = kb.sb([128, 128], F32)
        epst = kb.sb([128, 1], F32)
        sq = kb.sb([128, 4, T], F32)
        acc = kb.sb([128, T], F32); accp = kb.sb([128, T], F32)
        rstd = kb.sb([128, T], F32)
        tA = [kb.sb([128, T], F32) for _ in range(4)]
        tB = [kb.sb([128, T], F32) for _ in range(4)]
        psb = [kb.ps([128, 512]) for _ in range(8)]

        for (t_, d_, n_) in ((gm, gm_d, 'gm'), (gf, gf_d, 'gf'), (gl, gl_d, 'gl'), (cw, cw_d, 'cw'), (cb, cb_d, 'cb')):
            kb.dma('sp', t_[:], d_[:, :], writes=[n_])
        kb.op('pool', lambda e: e.memset(ones[:], 1.0), writes=['ones'])
        kb.op('pool', lambda e: e.memset(epst[:], eps), writes=['eps'])

        wctr = [0]

        def load_w(W, k0, nk, n0, ncols):
            i = wctr[0] % NW
            wctr[0] += 1
            src = W[k0 * 128:(k0 + nk) * 128, n0:n0 + ncols].rearrange("(k p) n -> p k n", p=128)
            kb.dma('pool', wbuf[i][:, 0:nk, 0:ncols], src, writes=[f'w{i}'])
            return wbuf[i], f'w{i}'

        pctr = [0]

        def gemmT(W, KC, n0, ncols, rhs_fn, rhs_res, epi):
            ntl = ncols // 128
            base = (pctr[0] % 2) * 4
            pctr[0] += 1
            for k0 in range(0, KC, NKW):
                nk = min(NKW, KC - k0)
                wt, wres = load_w(W, k0, nk, n0, ncols)
                for kk in range(nk):
                    k = k0 + kk
                    for nt in range(ntl):
                        kb.op('pe', lambda e, wt=wt, kk=kk, nt=nt, k=k, b=base + nt: e.matmul(
                            psb[b][:, 0:T], lhsT=wt[:, kk, nt * 128:(nt + 1) * 128], rhs=rhs_fn(k),
                            start=(k == 0), stop=(k == KC - 1)),
                            reads=[wres] + rhs_res, writes=[f'ps{base + nt}'])
            for nt in range(ntl):
                epi(nt, psb[base + nt], f'ps{base + nt}')

        def rmsnorm(gain, gres, src_res, dst_res):
            for p0 in range(0, KD, 4):
                npc = min(4, KD - p0)
                kb.op('act', lambda e, p0=p0, npc=npc: e.activation(out=sq[:, 0:npc, :], in_=hT[:, p0:p0 + npc, :], func=AF.Square),
                      reads=[src_res], writes=['sq'])
                dst = acc if p0 == 0 else accp
                kb.op('dve', lambda e, npc=npc, dst=dst: e.tensor_reduce(
                    out=dst[:], in_=sq[:, 0:npc, :].rearrange("p k t -> p t k"), axis=AX.X, op=ALU.add),
                    reads=['sq'], writes=['acc' if p0 == 0 else 'accp'])
                if p0 > 0:
                    kb.op('dve', lambda e: e.tensor_tensor(out=acc[:], in0=acc[:], in1=accp[:], op=ALU.add),
                          reads=['acc', 'accp'], writes=['acc'])
            kb.op('pe', lambda e: e.matmul(psb[0][:, 0:T], lhsT=ones[:], rhs=acc[:], start=True, stop=True),
                  reads=['ones', 'acc'], writes=['ps0'])
            kb.op('act', lambda e: e.activation(out=rstd[:], in_=psb[0][:, 0:T], func=AF.Sqrt, bias=epst[:], scale=1.0 / D),
                  reads=['ps0', 'eps'], writes=['rstd'])
            kb.op('dve', lambda e: e.reciprocal(out=rstd[:], in_=rstd[:]), reads=['rstd'], writes=['rstd'])

        for g in range(NG):
            kb.dma('sp', hT[:], hT_d[g].rearrange("(k p) t -> p k t", p=128), writes=['hT'])
            kb.dma('sp', yaT, ya_d[g].rearrange("(k p) t -> p k t", p=128), writes=['yaT', 'actT'])
            kb.dma('sp', zbT, zb_d[g].rearrange("(k p) t -> p k t", p=128), writes=['zbT', 'actT'])
            rmsnorm(gm, 'gm', 'hT', 'uT')
            for k in range(KD):
                kb.op('dve', lambda e, k=k: e.scalar_tensor_tensor(out=uT[:, k, :], in0=hT[:, k, :], scalar=gm[:, k:k + 1],
                                                                   in1=rstd[:], op0=ALU.mult, op1=ALU.mult),
                      reads=['hT', 'gm', 'rstd'], writes=['uT'])
            for n0 in range(0, D, 512):
                ncols = min(512, D - n0)

                def epi_sig(tl):
                    def f(nt, ps, pres):
                        kb.op('act', lambda e: e.activation(out=tl[nt][:], in_=ps[:, 0:T], func=AF.Sigmoid),
                              reads=[pres], writes=[f'{id(tl)}_{nt}'])
                    return f

                def epi_mulA(nt, ps, pres):
                    kb.op('dve', lambda e: e.tensor_tensor(out=tA[nt][:], in0=ps[:, 0:T], in1=tA[nt][:], op=ALU.mult),
                          reads=[pres, f'{id(tA)}_{nt}'], writes=[f'{id(tA)}_{nt}'])

                def epi_mulB(nt, ps, pres, n0=n0):
                    kb.op('dve', lambda e: e.tensor_tensor(out=tB[nt][:], in0=ps[:, 0:T], in1=tB[nt][:], op=ALU.mult),
                          reads=[pres, f'{id(tB)}_{nt}'], writes=[f'{id(tB)}_{nt}'])
                    kb.op('dve', lambda e: e.tensor_tensor(out=mixT[:, n0 // 128 + nt, :], in0=tA[nt][:], in1=tB[nt][:], op=ALU.add),
                          reads=[f'{id(tA)}_{nt}', f'{id(tB)}_{nt}'], writes=['mixT', 'actT'])

                gemmT(wg_d, KD, n0, ncols, lambda k: uT[:, k, :], ['uT'], epi_sig(tA))
                gemmT(wa_d, KA, n0, ncols, lambda k: yaT[:, k, :], ['yaT'], epi_mulA)
                gemmT(wg_d, KD, D + n0, ncols, lambda k: uT[:, k, :], ['uT'], epi_sig(tB))
                gemmT(wb_d, KA, n0, ncols, lambda k: zbT[:, k, :], ['zbT'], epi_mulB)
            for n0 in range(0, D, 512):
                ncols = min(512, D - n0)

                def epi_res(nt, ps, pres, n0=n0):
                    kb.op('dve', lambda e: e.tensor_tensor(out=hT[:, n0 // 128 + nt, :], in0=ps[:, 0:T], in1=hT[:, n0 // 128 + nt, :], op=ALU.add),
                          reads=[pres, 'hT'], writes=['hT'])
                gemmT(wo_d, KD, n0, ncols, lambda k: mixT[:, k, :], ['mixT'], epi_res)
            rmsnorm(gf, 'gf', 'hT', 'uT')
            for k in range(KD):
                kb.op('dve', lambda e, k=k: e.scalar_tensor_tensor(out=uT[:, k, :], in0=hT[:, k, :], scalar=gf[:, k:k + 1],
                                                                   in1=rstd[:], op0=ALU.mult, op1=ALU.mult),
                      reads=['hT', 'gf', 'rstd'], writes=['uT'])
            for n0 in range(0, DFF, 512):
                ncols = min(512, DFF - n0)

                def epi_gate(nt, ps, pres, n0=n0):
                    f = n0 // 128 + nt
                    kb.op('act', lambda e: e.copy(out=tA[nt][:], in_=ps[:, 0:T]), reads=[pres], writes=[f'ga{nt}'])
                    kb.op('dve', lambda e: e.tensor_scalar(out=tB[nt][:, 1:T - 1], in0=tA[nt][:, 1:T - 1], scalar1=cw[:, 3 * f + 1:3 * f + 2],
                                                           scalar2=cb[:, f:f + 1], op0=ALU.mult, op1=ALU.add),
                          reads=[f'ga{nt}', 'cw', 'cb'], writes=[f'gb{nt}'])
                    kb.op('dve', lambda e: e.scalar_tensor_tensor(out=tB[nt][:, 1:T - 1], in0=tA[nt][:, 0:T - 2], scalar=cw[:, 3 * f:3 * f + 1],
                                                                  in1=tB[nt][:, 1:T - 1], op0=ALU.mult, op1=ALU.add),
                          reads=[f'ga{nt}', f'gb{nt}'], writes=[f'gb{nt}'])
                    kb.op('dve', lambda e: e.scalar_tensor_tensor(out=tB[nt][:, 1:T - 1], in0=tA[nt][:, 2:T], scalar=cw[:, 3 * f + 2:3 * f + 3],
                                                                  in1=tB[nt][:, 1:T - 1], op0=ALU.mult, op1=ALU.add),
                          reads=[f'ga{nt}', f'gb{nt}'], writes=[f'gb{nt}'])
                    kb.op('act', lambda e: e.activation(out=tB[nt][:, 1:T - 1], in_=tB[nt][:, 1:T - 1], func=AF.Silu),
                          reads=[f'gb{nt}'], writes=[f'gb{nt}'])

                def epi_up(nt, ps, pres, n0=n0):
                    f = n0 // 128 + nt
                    kb.op('dve', lambda e: e.tensor_tensor(out=actT[:, f, 1:T - 1], in0=ps[:, 1:T - 1], in1=tB[nt][:, 1:T - 1], op=ALU.mult),
                          reads=[pres, f'gb{nt}'], writes=['actT', 'mixT', 'yaT', 'zbT'])
                gemmT(wfg_d, KD, n0, ncols, lambda k: uT[:, k, :], ['uT'], epi_gate)
                gemmT(wfu_d, KD, n0, ncols, lambda k: uT[:, k, :], ['uT'], epi_up)
            for n0 in range(0, D, 512):
                ncols = min(512, D - n0)

                def epi_res2(nt, ps, pres, n0=n0):
                    kb.op('dve', lambda e: e.tensor_tensor(out=hT[:, n0 // 128 + nt, 1:T - 1], in0=ps[:, 1:T - 1],
                                                           in1=hT[:, n0 // 128 + nt, 1:T - 1], op=ALU.add),
                          reads=[pres, 'hT'], writes=['hT'])
                gemmT(wfd_d, KF, n0, ncols, lambda k: actT[:, k, :], ['actT'], epi_res2)
            rmsnorm(gl, 'gl', 'hT', 'hT')
            for k in range(KD):
                kb.op('dve', lambda e, k=k: e.scalar_tensor_tensor(out=hT[:, k, :], in0=hT[:, k, :], scalar=gl[:, k:k + 1],
                                                                   in1=rstd[:], op0=ALU.mult, op1=ALU.mult),
                      reads=['hT', 'gl', 'rstd'], writes=['hT'])
            kb.dma('sp', out_d[g].rearrange("(k p) t -> p k t", p=128), hT[:, :, 1:T - 1], reads=['hT'], is_output=True)
        kb.emit()
    return nc


D_MODEL = 4096; SEQ = 16384; N_META = 16; GRID_W = 64; D_FF = 11008
AW = 2048; KVW = 512
NCORES = 8


def _pk(v, k):
    return np.ascontiguousarray(np.asarray(v, np.float32).reshape(k, 128).T)


def _rope_tables(L):
    rows = SEQ // GRID_W
    row = np.repeat(np.arange(rows, dtype=np.float32), GRID_W)
    col = np.tile(np.arange(GRID_W, dtype=np.float32), rows)
    meta = np.zeros((N_META,), np.float32)
    row = np.concatenate([meta, row]); col = np.concatenate([meta, col])
    inv_freq = (np.float32(10000.0) ** (-np.arange(32, dtype=np.float32) * np.float32(2.0) / np.float32(64))).astype(np.float32)
    ar = (row[:, None] * inv_freq).astype(np.float32); ac = (col[:, None] * inv_freq).astype(np.float32)
    cosT = np.concatenate([np.cos(ar), np.cos(ar), np.cos(ac), np.cos(ac)], 1).T
    sinT = np.concatenate([np.sin(ar), np.sin(ar), np.sin(ac), np.sin(ac)], 1).T
    return np.ascontiguousarray(cosT, np.float32), np.ascontiguousarray(sinT, np.float32)


def _rT():
    R = np.zeros((128, 128), np.float32)
    for a in range(2):
        for m in range(32):
            R[64 * a + m, 64 * a + 32 + m] = -1.0
            R[64 * a + 32 + m, 64 * a + m] = 1.0
    return np.ascontiguousarray(R.T)


def kernel(x, meta_tokens, norm_mix, w_in, q_norm, k_norm, hyena_conv_w, hyena_conv_b,
           filt_w1, filt_b1, filt_w2, filt_b2, filt_w3, filt_b3, filt_w4, filt_freq,
           hyena_skip, w_attn_branch, w_hyena_branch, w_out, norm_ffn, w_ffn_gate,
           w_ffn_up, ffn_conv_w, ffn_conv_b, w_ffn_down, norm_final):
    f32 = lambda a: np.asarray(a, np.float32)
    L = SEQ + N_META
    D = D_MODEL
    h = np.concatenate([f32(meta_tokens), f32(x)[0]], 0)
    hT = np.ascontiguousarray(h.T)
    hTg = layout_hT(hT)
    w_in0 = f32(w_in)[0]
    cosT, sinT = _rope_tables(L)
    consts = hy_consts(L)
    rT = _rT()
    gmix = _pk(f32(norm_mix)[0], D // 128)
    qkn = np.ascontiguousarray(np.stack([f32(q_norm)[0], f32(k_norm)[0]], 1))
    fbf = np.ascontiguousarray(np.stack([f32(filt_b1)[0], f32(filt_b2)[0], f32(filt_b3)[0], f32(filt_freq)[0]], 1))
    hcw_full = f32(hyena_conv_w)[0]; hcb_full = f32(hyena_conv_b)[0]; w4 = f32(filt_w4)[0]; skp = f32(hyena_skip)[0]
    HYB = AW + 2 * KVW
    in_a = []
    for c in range(NCORES):
        kv = c // 2
        cols = [np.arange(256 * c, 256 * c + 256), np.arange(AW + 128 * kv, AW + 128 * kv + 128),
                np.arange(AW + KVW + 128 * kv, AW + KVW + 128 * kv + 128)]
        bases = []
        for sec in range(3):
            for ct in range(2):
                bases.append(sec * 2048 + 256 * c + 128 * ct)
                cols.append(HYB + bases[-1] + np.arange(128))
        cols = np.concatenate(cols)
        d = dict(consts)
        d["hT"] = hTg
        d["w"] = np.ascontiguousarray(w_in0[:, cols])
        d["gmix"] = gmix; d["qkn"] = qkn; d["cosT"] = cosT; d["sinT"] = sinT; d["rT"] = rT
        d["nabsd"] = hy_delta(c)
        d["fw1"] = f32(filt_w1)[0]; d["fw2"] = f32(filt_w2)[0]; d["fw3"] = f32(filt_w3)[0]
        w4c = [w4[:, o * 4096 + dd * 2048 + 256 * c + 128 * ct: o * 4096 + dd * 2048 + 256 * c + 128 * ct + 128]
               for o in range(2) for dd in range(2) for ct in range(2)]
        d["fw4"] = np.ascontiguousarray(np.concatenate(w4c, 1))
        d["fbf"] = fbf
        d["skip"] = np.ascontiguousarray(np.stack([skp[o, 256 * c + 128 * ct: 256 * c + 128 * ct + 128] for o in range(2) for ct in range(2)], 1))
        d["hcw"] = np.ascontiguousarray(np.concatenate([hcw_full[:, b:b + 128].T for b in bases], 1))
        d["hcb"] = np.ascontiguousarray(np.stack([hcb_full[b:b + 128] for b in bases], 1))
        in_a.append(d)
    nca = build_A(D, L, hyena=True)
    ra = run_bass_kernel_spmd(nca, in_a, core_ids=list(range(NCORES)))
    yaT = np.concatenate([np.asarray(ra.results[c]["yaT"]) for c in range(NCORES)], 0)
    zbT = np.concatenate([np.asarray(ra.results[c]["z2T"]) for c in range(NCORES)], 0)
    del in_a, ra

    NG, TI = 5, 410
    T = TI + 2
    TC = NG * TI
    assert TC * NCORES == L
    def _padcols(a):
        p = np.zeros((a.shape[0], a.shape[1] + 2), a.dtype)
        p[:, 1:-1] = a
        return p
    hTp = _padcols(hT); yap = _padcols(yaT); zbp = _padcols(zbT)
    KF = D_FF // 128
    cwf = f32(ffn_conv_w)[0]
    cw = np.ascontiguousarray(cwf.T.reshape(KF, 128, 3).transpose(1, 0, 2).reshape(128, KF * 3))
    common = dict(
        wg=np.ascontiguousarray(w_in0[:, HYB + 6144:]), wa=f32(w_attn_branch)[0], wb=f32(w_hyena_branch)[0], wo=f32(w_out)[0],
        wfg=f32(w_ffn_gate)[0], wfu=f32(w_ffn_up)[0], wfd=f32(w_ffn_down)[0],
        gmix=gmix, gffn=_pk(f32(norm_ffn)[0], D // 128), gfin=_pk(f32(norm_final), D // 128),
        cw=cw, cb=_pk(f32(ffn_conv_b)[0], KF))
    in_b = []
    for c in range(NCORES):
        d = dict(common)
        st = [TC * c + TI * j for j in range(NG)]
        d["hT"] = np.ascontiguousarray(np.stack([hTp[:, s:s + T] for s in st], 0))
        d["yaT"] = np.ascontiguousarray(np.stack([yap[:, s:s + T] for s in st], 0))
        d["zbT"] = np.ascontiguousarray(np.stack([zbp[:, s:s + T] for s in st], 0))
        in_b.append(d)
    ncb = build_B(D, AW, D_FF, NG, TI)
    rb = run_bass_kernel_spmd(ncb, in_b, core_ids=list(range(NCORES)))
    out = np.empty((L, D), np.float32)
    for c in range(NCORES):
        o = np.asarray(rb.results[c]["outT"])
        for j in range(NG):
            s = TC * c + TI * j
            out[s:s + TI] = o[j].T
    return np.ascontiguousarray(out[N_META:][None])
```
